# Optimizing a Trainium2 kernel written in Bass

```python
import jax, jax.numpy as jnp
from jax import lax
import numpy as np

D_MODEL = 1024
BATCH = 8
SEQ = 4096
DEPTH = 1
DEC_BATCH = 16
DEC_SEQ = 32
PAST_LEN = 4096

CHUNK = 64
MIX_WIDTH = D_MODEL
HG_WIDTH = MIX_WIDTH // 2
HG_HEADS = 4
HG_DIM = HG_WIDTH // HG_HEADS
HG_BLOCK = 16
HG_COLS = 4 * HG_WIDTH
RW_WIDTH = MIX_WIDTH - HG_WIDTH
RW_HEAD = 64
RW_HEADS = RW_WIDTH // RW_HEAD
RW_DECAY_LORA = 64
RW_A_LORA = 64
RW_GATE_LORA = 128
RW_COLS = 3 * RW_WIDTH + RW_DECAY_LORA + RW_A_LORA + RW_GATE_LORA
IN_COLS = HG_COLS + RW_COLS
N_MEM = 256
X_HEADS = 4
X_DIM = D_MODEL // X_HEADS
D_FF = -(-8 * D_MODEL // (3 * 256)) * 256
RMS_EPS = 1e-6
GN_EPS = 64e-5

kernel_name = "hgrn2_rwkv7_parallel_heads_stream_step"


def rmsnorm(x, g):
    xf = x.astype(jnp.float32)
    y = xf * lax.rsqrt(jnp.mean(xf * xf, axis=-1, keepdims=True) + RMS_EPS)
    return (y * g.astype(jnp.float32)).astype(x.dtype)


def hgrn2_chunkwise(q, logf, k, v, s0):
    bsz, t_len = q.shape[0], q.shape[1]
    pad = (-t_len) % HG_BLOCK

    def blocks(a):
        a = jnp.pad(a, ((0, 0), (0, pad), (0, 0), (0, 0)))
        return a.reshape(bsz, -1, HG_BLOCK, HG_HEADS, HG_DIM)

    q, logf, k, v = blocks(q), blocks(logf), blocks(k), blocks(v)
    b = jnp.cumsum(logf, axis=2)
    b_last = b[:, :, -1:]
    qg = q * jnp.exp(b)
    kg = k * jnp.exp(-b)
    kd = k * jnp.exp(b_last - b)
    causal = jnp.tril(jnp.ones((HG_BLOCK, HG_BLOCK), dtype=bool))
    att = jnp.where(causal, jnp.einsum('bnthk,bnshk->bnhts', qg, kg), 0.0)
    o_intra = jnp.einsum('bnhts,bnshv->bnthv', att, v)

    def step(S, xs):
        qg_n, kd_n, v_n, dec_n = xs
        o_n = jnp.einsum('bthk,bhkv->bthv', qg_n, S)
        S = dec_n[..., None] * S + jnp.einsum('bshk,bshv->bhkv', kd_n, v_n)
        return S, o_n

    blk_first = lambda a: jnp.moveaxis(a, 1, 0)
    s_final, o_inter = lax.scan(step, s0, (blk_first(qg), blk_first(kd), blk_first(v),
                                           blk_first(jnp.exp(b_last[:, :, 0]))))
    o = (o_intra + blk_first(o_inter)).reshape(bsz, -1, HG_HEADS, HG_DIM)[:, :t_len]
    return o, s_final


def hgrn2_mixer(p, lb, g_norm, s0):
    bsz, t_len = p.shape[0], p.shape[1]
    pf = p.astype(jnp.float32)
    q, fpre, i_in, g = jnp.split(pf, 4, axis=-1)
    lb = lb.astype(jnp.float32)
    logf = jnp.log(lb + (1.0 - lb) * jax.nn.sigmoid(fpre))
    k = (1.0 - lb) * jax.nn.sigmoid(-fpre)
    hv = lambda a: a.reshape(bsz, t_len, HG_HEADS, HG_DIM)
    o, s_final = hgrn2_chunkwise(hv(q), hv(logf), hv(k), hv(i_in), s0.astype(jnp.float32))
    o = o * lax.rsqrt(jnp.mean(o * o, axis=-1, keepdims=True) + RMS_EPS)
    o = o.reshape(bsz, t_len, HG_WIDTH) * g_norm.astype(jnp.float32) * jax.nn.silu(g)
    return o.astype(p.dtype), s_final.astype(p.dtype)


def rwkv7_mixer(p, shift0, s0, mu, w0, w_b, a0, a_b, g_b, k_k, k_a, r_k, gn_w, gn_b):
    bsz, t_len = p.shape[0], p.shape[1]
    prev = jnp.concatenate([shift0.astype(p.dtype), p[:, :-1]], axis=1)
    xs = (p + (prev - p) * mu).astype(jnp.float32)
    new_shift = p[:, -1:]
    cut = [RW_WIDTH, 2 * RW_WIDTH, 3 * RW_WIDTH, 3 * RW_WIDTH + RW_DECAY_LORA,
           3 * RW_WIDTH + RW_DECAY_LORA + RW_A_LORA]
    r, k, v, wd, ad, gd = jnp.split(xs, cut, axis=-1)
    w = -jax.nn.softplus(-(w0 + jnp.tanh(wd) @ w_b)) - 0.5
    decay = jnp.exp(-jnp.exp(w))
    a = jax.nn.sigmoid(a0 + ad @ a_b)
    g = jax.nn.sigmoid(gd) @ g_b
    kk = k * k_k
    k = k * (1.0 + (a - 1.0) * k_a)
    hv = lambda t: t.reshape(bsz, t_len, RW_HEADS, RW_HEAD)
    r, k, v, a, decay, kk = hv(r), hv(k), hv(v), hv(a), hv(decay), hv(kk)
    kk = kk / jnp.maximum(jnp.sqrt(jnp.sum(kk * kk, axis=-1, keepdims=True)), 1e-12)

    def step(S, xs_t):
        r_t, w_t, k_t, v_t, kk_t, a_t = xs_t
        sa = jnp.einsum('bhvk,bhk->bhv', S, -kk_t)
        S = (S * w_t[:, :, None, :] + sa[..., None] * (kk_t * a_t)[:, :, None, :]
             + v_t[..., None] * k_t[:, :, None, :])
        return S, jnp.einsum('bhvk,bhk->bhv', S, r_t)

    tm = lambda t: jnp.moveaxis(t, 1, 0)
    s_final, y = lax.scan(step, s0.astype(jnp.float32), (tm(r), tm(decay), tm(k), tm(v), tm(kk), tm(a)))
    y = jnp.moveaxis(y, 0, 1)
    mean = jnp.mean(y, axis=-1, keepdims=True)
    var = jnp.mean(jnp.square(y - mean), axis=-1, keepdims=True)
    y = ((y - mean) * lax.rsqrt(var + GN_EPS)).reshape(bsz, t_len, RW_WIDTH) * gn_w + gn_b
    bonus = jnp.sum(r * k * r_k, axis=-1, keepdims=True) * v
    y = (y + bonus.reshape(bsz, t_len, RW_WIDTH)) * g
    return y.astype(p.dtype), s_final.astype(p.dtype), new_shift


def memory_kv(mem, g_mem, w_k, w_v):
    bsz = mem.shape[0]
    nm = rmsnorm(mem, g_mem)
    mk = (nm @ w_k).reshape(bsz, N_MEM, X_HEADS, X_DIM)
    mv = (nm @ w_v).reshape(bsz, N_MEM, X_HEADS, X_DIM)
    return mk, mv


def cross_attend(n, mk, mv, w_q, w_o):
    bsz, t_len = n.shape[0], n.shape[1]
    q = (n @ w_q).reshape(bsz, t_len, X_HEADS, X_DIM).astype(jnp.float32)
    s = jnp.einsum('bthd,bmhd->bhtm', q, mk.astype(jnp.float32)) * (X_DIM ** -0.5)
    pr = jax.nn.softmax(s, axis=-1)
    o = jnp.einsum('bhtm,bmhd->bthd', pr, mv.astype(jnp.float32)).astype(n.dtype)
    return o.reshape(bsz, t_len, D_MODEL) @ w_o


def layer(x, mk, mv, hg_s0, rw_s0, shift0, lb, lw):
    n = rmsnorm(x, lw['norm_mix'])
    p = n @ lw['w_in']
    hg_o, hg_s = hgrn2_mixer(p[..., :HG_COLS], lb, lw['hgrn_norm'], hg_s0)
    rw_o, rw_s, rw_shift = rwkv7_mixer(p[..., HG_COLS:], shift0, rw_s0, lw['rw_mu'], lw['rw_w0'], lw['rw_w_b'],
                                       lw['rw_a0'], lw['rw_a_b'], lw['rw_g_b'], lw['rw_k_k'], lw['rw_k_a'],
                                       lw['rw_r_k'], lw['rw_gn_w'], lw['rw_gn_b'])
    x = x + jnp.concatenate([hg_o, rw_o], axis=-1) @ lw['w_out']
    x = x + cross_attend(rmsnorm(x, lw['norm_cross']), mk, mv, lw['w_cq'], lw['w_co'])
    n = rmsnorm(x, lw['norm_ffn'])
    x = x + (jax.nn.silu(n @ lw['w_ff1']) * (n @ lw['w_ff3'])) @ lw['w_ff2']
    return x, hg_s, rw_s, rw_shift


def setup_inputs(seed: int = 0) -> dict:
    key = jax.random.key(seed)
    ks = jax.random.split(key, 40)
    nrm = lambda i, shape, scale: jax.random.normal(ks[i], shape, jnp.float32) * scale
    D = D_MODEL
    return {
        'x_prompt': nrm(0, (BATCH, SEQ, D), 1.0),
        'mem_prompt': nrm(1, (BATCH, N_MEM, D), 1.0),
        'x_sample': nrm(2, (DEC_BATCH, DEC_SEQ, D), 1.0),
        'cache_mem_k': nrm(3, (DEPTH, DEC_BATCH, N_MEM, X_HEADS, X_DIM), 1.0),
        'cache_mem_v': nrm(4, (DEPTH, DEC_BATCH, N_MEM, X_HEADS, X_DIM), 1.0),
        'state_hgrn': nrm(5, (DEPTH, DEC_BATCH, HG_HEADS, HG_DIM, HG_DIM), 0.5),
        'state_rwkv': nrm(6, (DEPTH, DEC_BATCH, RW_HEADS, RW_HEAD, RW_HEAD), 0.5),
        'state_rwkv_shift': nrm(7, (DEPTH, DEC_BATCH, 1, RW_COLS), 1.0),
        'hgrn_lb_logits': nrm(8, (DEPTH + 1, HG_WIDTH), 0.1),
        'norm_mix': 1.0 + nrm(9, (DEPTH, D), 0.02),
        'w_in': nrm(10, (DEPTH, D, IN_COLS), D ** -0.5),
        'hgrn_norm': 1.0 + nrm(11, (DEPTH, HG_WIDTH), 0.02),
        'rw_mu': jax.random.uniform(ks[12], (DEPTH, RW_COLS), jnp.float32),
        'rw_w0': jax.random.uniform(ks[13], (DEPTH, RW_WIDTH), jnp.float32, -2.5, 0.5),
        'rw_w_b': nrm(14, (DEPTH, RW_DECAY_LORA, RW_WIDTH), 0.1),
        'rw_a0': nrm(15, (DEPTH, RW_WIDTH), 0.1),
        'rw_a_b': nrm(16, (DEPTH, RW_A_LORA, RW_WIDTH), 0.1),
        'rw_g_b': nrm(17, (DEPTH, RW_GATE_LORA, RW_WIDTH), RW_GATE_LORA ** -0.5),
        'rw_k_k': 0.85 + nrm(18, (DEPTH, RW_WIDTH), 0.02),
        'rw_k_a': 1.0 + nrm(19, (DEPTH, RW_WIDTH), 0.02),
        'rw_r_k': nrm(20, (DEPTH, RW_HEADS, RW_HEAD), 0.1),
        'rw_gn_w': 1.0 + nrm(21, (DEPTH, RW_WIDTH), 0.02),
        'rw_gn_b': nrm(22, (DEPTH, RW_WIDTH), 0.01),
        'w_out': nrm(23, (DEPTH, MIX_WIDTH, D), MIX_WIDTH ** -0.5),
        'norm_cross': 1.0 + nrm(24, (DEPTH, D), 0.02),
        'norm_mem': 1.0 + nrm(25, (DEPTH, D), 0.02),
        'w_cq': nrm(26, (DEPTH, D, D), D ** -0.5),
        'w_ck': nrm(27, (DEPTH, D, D), D ** -0.5),
        'w_cv': nrm(28, (DEPTH, D, D), D ** -0.5),
        'w_co': nrm(29, (DEPTH, D, D), D ** -0.5),
        'norm_ffn': 1.0 + nrm(30, (DEPTH, D), 0.02),
        'w_ff1': nrm(31, (DEPTH, D, D_FF), D ** -0.5),
        'w_ff3': nrm(32, (DEPTH, D, D_FF), D ** -0.5),
        'w_ff2': nrm(33, (DEPTH, D_FF, D), D_FF ** -0.5),
        'norm_final': 1.0 + nrm(34, (D,), 0.02),
    }


def reference(x_prompt, mem_prompt, x_sample, cache_mem_k, cache_mem_v, state_hgrn, state_rwkv, state_rwkv_shift,
              hgrn_lb_logits, norm_mix, w_in, hgrn_norm, rw_mu, rw_w0, rw_w_b, rw_a0, rw_a_b, rw_g_b, rw_k_k, rw_k_a,
              rw_r_k, rw_gn_w, rw_gn_b, w_out, norm_cross, norm_mem, w_cq, w_ck, w_cv, w_co, norm_ffn, w_ff1, w_ff3,
              w_ff2, norm_final):
    lb_table = jnp.cumsum(jax.nn.softmax(hgrn_lb_logits.astype(jnp.float32), axis=0), axis=0)
    bp = x_prompt.shape[0]
    xp, xs = x_prompt, x_sample
    p_hg, p_rw, p_sh, p_mk, p_mv, s_hg, s_rw, s_sh = [], [], [], [], [], [], [], []
    for l in range(DEPTH):
        lw = {'norm_mix': norm_mix[l], 'w_in': w_in[l], 'hgrn_norm': hgrn_norm[l], 'rw_mu': rw_mu[l],
              'rw_w0': rw_w0[l], 'rw_w_b': rw_w_b[l], 'rw_a0': rw_a0[l], 'rw_a_b': rw_a_b[l], 'rw_g_b': rw_g_b[l],
              'rw_k_k': rw_k_k[l], 'rw_k_a': rw_k_a[l], 'rw_r_k': rw_r_k[l], 'rw_gn_w': rw_gn_w[l],
              'rw_gn_b': rw_gn_b[l], 'w_out': w_out[l], 'norm_cross': norm_cross[l], 'w_cq': w_cq[l],
              'w_co': w_co[l], 'norm_ffn': norm_ffn[l], 'w_ff1': w_ff1[l], 'w_ff3': w_ff3[l], 'w_ff2': w_ff2[l]}
        lb = lb_table[l]
        mk, mv = memory_kv(mem_prompt, norm_mem[l], w_ck[l], w_cv[l])
        hg0 = jnp.zeros((bp, HG_HEADS, HG_DIM, HG_DIM), xp.dtype)
        rw0 = jnp.zeros((bp, RW_HEADS, RW_HEAD, RW_HEAD), xp.dtype)
        sh0 = jnp.zeros((bp, 1, RW_COLS), xp.dtype)
        xp, hg_new, rw_new, sh_new = layer(xp, mk, mv, hg0, rw0, sh0, lb, lw)
        p_hg.append(hg_new); p_rw.append(rw_new); p_sh.append(sh_new); p_mk.append(mk); p_mv.append(mv)
        xs, hg_new, rw_new, sh_new = layer(xs, cache_mem_k[l], cache_mem_v[l], state_hgrn[l], state_rwkv[l],
                                           state_rwkv_shift[l], lb, lw)
        s_hg.append(hg_new); s_rw.append(rw_new); s_sh.append(sh_new)
    y_prompt = rmsnorm(xp, norm_final)
    y_sample = rmsnorm(xs, norm_final)
    return (y_prompt, y_sample, jnp.stack(p_hg), jnp.stack(p_rw), jnp.stack(p_sh), jnp.stack(p_mk),
            jnp.stack(p_mv), jnp.stack(s_hg), jnp.stack(s_rw), jnp.stack(s_sh))
```

```python
from contextlib import ExitStack
import os as _os
import numpy as np
import concourse.bass as bass
import concourse.mybir as mybir
from concourse.bass_utils import run_bass_kernel_spmd

F32 = mybir.dt.float32
BF16 = mybir.dt.bfloat16
AF = mybir.ActivationFunctionType
ALU = mybir.AluOpType

D = 1024
DFF = 2816
NMEM = 256
RWC = 1792
RMS_EPS = 1e-6
GN_EPS = 64e-5
DECAY_K = 0.6065306597126334


class _Op:
    __slots__ = ("eng", "fn", "deps", "dma", "semkey", "sem", "val", "signal", "ph")

    def __init__(self, eng, fn, dma, semkey):
        self.eng = eng
        self.fn = fn
        self.deps = []
        self.dma = dma
        self.semkey = semkey
        self.sem = None
        self.val = 0
        self.signal = False


class Sched:
    ENGS = ("pe", "act", "dve", "pool", "sp")

    def __init__(self, nc, es):
        self.nc = nc
        self.es = es
        self.ops = []
        self.res = {}
        self.bases = {}
        self.ranges = {}
        self.dma_cnt = {}
        self.stores = []
        self.phase = ""

    def set_range(self, base, lo, hi):
        self.ranges[base] = (lo, hi)

    def _conf(self, tok):
        base = tok.split("#")[0]
        out = []
        whole = ("#" not in tok) or base.startswith("ps")
        for t in self.bases.get(base, ()):
            if t == tok or whole or "#" not in t:
                out.append(t)
        rg = self.ranges.get(base)
        if rg is not None:
            for b2, r2 in self.ranges.items():
                if b2 != base and r2[0] < rg[1] and rg[0] < r2[1]:
                    out.extend(self.bases.get(b2, ()))
        return out

    def _dep(self, op, prod, raw):
        if prod is op or prod is None:
            return
        if not op.dma and not prod.dma and op.eng == prod.eng:
            if op.eng == "pe":
                return
        if prod not in op.deps:
            op.deps.append(prod)

    def op(self, eng, fn, reads=(), writes=(), dma=False, semkey=None, store=False):
        o = _Op(eng, fn, dma, semkey)
        o.ph = self.phase
        psr = [r for r in reads if r.startswith("ps")]
        if psr:
            reads = [r for r in reads if not r.startswith("ps")]
            writes = list(writes) + [r for r in psr if r not in writes]
            o_psr = True
        else:
            o_psr = False
        for r in reads:
            for t in self._conf(r):
                st = self.res.get(t)
                if st is not None:
                    self._dep(o, st[0], True)
        for w in writes:
            isps = w.startswith("ps")
            for t in self._conf(w):
                st = self.res.get(t)
                if st is not None:
                    self._dep(o, st[0], isps and o_psr)
                    for rd in st[1]:
                        self._dep(o, rd, False)
        for r in reads:
            st = self.res.get(r)
            if st is None:
                st = self.res[r] = [None, []]
                self.bases.setdefault(r.split("#")[0], set()).add(r)
            if not dma:
                st[1] = [x for x in st[1] if x.dma or x.eng != eng]
            st[1].append(o)
        for w in writes:
            self.bases.setdefault(w.split("#")[0], set()).add(w)
            self.res[w] = [o, []]
        if dma:
            o.signal = True
        self.ops.append(o)
        if store:
            self.stores.append(o)
        return o

    def i(self, eng, name, reads, writes, **kw):
        def fn(e, name=name, kw=kw):
            return getattr(e, name)(**kw)
        return self.op(eng, fn, reads, writes)

    def dma(self, q, out, in_, reads=(), writes=(), semkey=None, store=False, **kw):
        return self.op(q, lambda e: e.dma_start(out=out, in_=in_, **kw), reads, writes,
                       dma=True, semkey=semkey, store=store)

    def emit(self):
        nc, es = self.nc, self.es
        for o in self.ops:
            for p in o.deps:
                p.signal = True
        esem = {e: es.enter_context(nc.semaphore("s_" + e)) for e in self.ENGS}
        dsem = {}
        cnt = {e: 0 for e in self.ENGS}
        for o in self.ops:
            if o.dma:
                if o.semkey not in dsem:
                    dsem[o.semkey] = es.enter_context(nc.semaphore("d_%d" % len(dsem)))
                    self.dma_cnt[o.semkey] = 0
                self.dma_cnt[o.semkey] += 16
                o.sem = dsem[o.semkey]
                o.val = self.dma_cnt[o.semkey]
            elif o.signal:
                cnt[o.eng] += 1
                o.sem = esem[o.eng]
                o.val = cnt[o.eng]
        per = {e: [o for o in self.ops if o.eng == e] for e in self.ENGS}
        if _os.environ.get("MK_PHASES"):
            import json
            json.dump({e: [o.ph for o in per[e] if not o.dma] for e in self.ENGS}, open(_os.environ["MK_PHASES"], "w"))
        finals = {}
        for o in self.ops:
            if o.dma:
                finals[id(o.sem)] = (o.sem, max(finals.get(id(o.sem), (None, 0))[1], o.val))

        def run(e, h, last=False):
            seen = {}
            for o in per[e]:
                for p in o.deps:
                    k = id(p.sem)
                    if seen.get(k, 0) < p.val:
                        h.wait_ge(p.sem, p.val)
                        seen[k] = p.val
                inst = o.fn(h)
                if o.signal:
                    inst.then_inc(o.sem, 16 if o.dma else 1)
            if last:
                for sem, v in finals.values():
                    h.wait_ge(sem, v)

        with nc.Block() as block:
            @block.tensor
            def _(h):
                run("pe", h)

            @block.scalar
            def _(h):
                run("act", h)

            @block.vector
            def _(h):
                run("dve", h)

            @block.gpsimd
            def _(h):
                run("pool", h)

            @block.sync
            def _(h):
                run("sp", h, last=True)
        return {e: len(per[e]) for e in self.ENGS}, len(dsem)


VP_COLS = {}
_off = 0
for _n, _c in (("g_mix", 8), ("g_cross", 8), ("g_ffn", 8), ("g_mem", 8), ("lb0", 4), ("lb1", 4), ("mu", 14),
               ("w0", 4), ("a0", 4), ("kk", 4), ("ka", 4), ("rk", 4)):
    VP_COLS[_n] = _off
    _off += _c
NVP_IN = _off
for _n, _c in (("lb", 4), ("oml", 4), ("noml", 4), ("nw0", 4), ("na0", 4), ("omka", 4)):
    VP_COLS[_n] = _off
    _off += _c
NVP = _off
VB_COLS = {"hgn": 0, "gnw": 512, "gnb": 1024, "nf": 1536}
NVB = 2560


def _chunks(v):
    v = np.asarray(v, np.float32).reshape(-1, 128)
    return np.ascontiguousarray(v.T)


def make_masks():
    m = {}
    p = np.arange(128)[:, None]
    q = np.arange(128)[None, :]
    for C in (64, 32):
        HC = 2 * C
        m["hg%d" % C] = (((p // C) == (q // C)) & (p <= q)).astype(np.float32)
        inb = (p < HC) & (q < HC) & ((p // C) == (q // C))
        t, u = p % C, q % C
        m["a1_%d" % C] = -(inb & (t > u)).astype(np.float32)
        j = np.arange(C)[None, :]
        su = (inb & (t < u)).astype(np.float32)[:, 0:HC]
        iu = ((t <= j) & (p < HC)).astype(np.float32) * np.ones((128, C), np.float32)
        m["a2_%d" % C] = np.concatenate([-su, iu], 1)
        m["a3_%d" % C] = np.concatenate([su, iu], 1)
    return m


def build_program(NMT=8, sample=True, dbg=(), stop_after=None):
    nc = bass.Bass("TRN2", target_bir_lowering=False)
    dt_in = lambda n, s: nc.dram_tensor(n, list(s), F32, kind="ExternalInput").ap()
    dt_out = lambda n, s: nc.dram_tensor(n, list(s), F32, kind="ExternalOutput").ap()
    TPROMPT = 512 * NMT
    I = dict(
        xp=dt_in("xp", (TPROMPT, D)), xs=dt_in("xs", (64, D)), mem=dt_in("mem", (NMEM, D)),
        ck=dt_in("ck", (2, NMEM, D)), cv=dt_in("cv", (2, NMEM, D)),
        sh_hg=dt_in("sh_hg", (2, 4, 128, 128)), sh_rw=dt_in("sh_rw", (2, 8, 64, 64)), sh_sh=dt_in("sh_sh", (128, 28)),
        w_in=dt_in("w_in", (D, 3840)), w_out=dt_in("w_out", (D, D)), w_cq=dt_in("w_cq", (D, D)),
        w_ck=dt_in("w_ck", (D, D)), w_cv=dt_in("w_cv", (D, D)), w_co=dt_in("w_co", (D, D)),
        w_ff1=dt_in("w_ff1", (D, DFF)), w_ff3=dt_in("w_ff3", (D, DFF)), w_ff2=dt_in("w_ff2", (DFF, D)),
        w_b=dt_in("w_b", (64, 512)), a_b=dt_in("a_b", (64, 512)), g_b=dt_in("g_b", (128, 512)),
        vp=dt_in("vp", (128, NVP_IN)), vb=dt_in("vb", (1, NVB)),
        ident=dt_in("ident", (128, 128)), e64=dt_in("e64", (128, 128)), ehead=dt_in("ehead", (128, 2)),
        rmask64=dt_in("rmask64", (1, 512)), rmask32=dt_in("rmask32", (1, 64)),
    )
    masks = make_masks()
    for k, v in masks.items():
        I["m_" + k] = dt_in("m_" + k, v.shape)
    O = dict(
        yp=dt_out("yp", (TPROMPT, D)), ys=dt_out("ys", (64, D)),
        o_p_hg=dt_out("o_p_hg", (4, 128, 128)), o_p_rw=dt_out("o_p_rw", (8, 64, 64)), o_p_sh=dt_out("o_p_sh", (128, 14)),
        o_mk=dt_out("o_mk", (NMEM, D)), o_mv=dt_out("o_mv", (NMEM, D)),
        o_s_hg=dt_out("o_s_hg", (2, 4, 128, 128)), o_s_rw=dt_out("o_s_rw", (2, 8, 64, 64)),
        o_s_sh=dt_out("o_s_sh", (128, 28)),
    )
    DBG = {}
    for name, shape in dbg:
        DBG[name] = dt_out("dbg_" + name, shape)

    es = ExitStack()
    with es:
        S = Sched(nc, es)
        sbt = lambda n, s, d: es.enter_context(nc.sbuf_tensor("sb_" + n, list(s), d))

        xres = [sbt("xres%d" % i, (128, 4, D), F32) for i in range(2)]
        NWS = 3
        wsl = [sbt("wsl%d" % i, (128, 8, 512), BF16) for i in range(NWS)]
        fm = [sbt("fm%d" % i, (128, 8, 512), BF16) for i in range(2)]
        ntm = [sbt("ntm%d" % i, (128, D), BF16) for i in range(2)]
        mixtm = sbt("mixtm", (128, 4, D), BF16)
        vp = sbt("vp", (128, NVP), F32)
        vbb = sbt("vbb", (128, NVB), F32)
        identb = sbt("identb", (128, 128), BF16)
        identf = sbt("identf", (128, 128), F32)
        e64 = sbt("e64", (128, 128), BF16)
        ehead = sbt("ehead", (128, 2), BF16)
        wlo = sbt("wlo", (128, 2, 512), BF16)
        rmask = {64: sbt("rmask64", (128, 512), F32), 32: sbt("rmask32", (128, 64), F32)}
        mk = {}
        for k, v in masks.items():
            mk[k] = sbt("m_" + k, (128, v.shape[1]), BF16)
        stat = sbt("stat", (128, 64), F32)
        carry = sbt("carry", (128, 14), F32)
        shs = sbt("shs", (128, 28), F32)
        sho = sbt("sho", (128, 28), F32)
        S_hg = [sbt("S_hg%d" % i, (128, 4, 128), F32) for i in range(3)]
        S_hgb = [sbt("S_hgb%d" % i, (128, 4, 128), BF16) for i in range(3)]
        H_rw = [sbt("H_rw%d" % i, (128, 4, 128), F32) for i in range(3)]
        H_rwb = [sbt("H_rwb%d" % i, (128, 4, 128), BF16) for i in range(3)]
        mkfm = sbt("mkfm", (128, 8, NMEM), BF16)
        mvtm = sbt("mvtm", (128, 2, D), BF16)

        SA = nc.sbuf_bytes_remaining - 64
        SA -= SA % 64
        arena = sbt("arena", (128, SA // 2), BF16)
        bump = [0]

        def aalloc(name, shape, dtype):
            n = int(np.prod(shape))
            nb = n * (4 if dtype == F32 else 2)
            lo = bump[0]
            bump[0] = lo + ((nb + 31) // 32) * 32
            assert bump[0] <= SA, (name, bump[0], SA)
            S.set_range(name, lo, lo + nb)
            v = arena[:, lo // 2: lo // 2 + nb // 2]
            if dtype == F32:
                v = v.bitcast(F32)
            if len(shape) == 2:
                v = v.rearrange("p (a b) -> p a b", a=shape[0])
            elif len(shape) == 3:
                v = v.rearrange("p (a b c) -> p a b c", a=shape[0], b=shape[1])
            return v

        ps = [es.enter_context(nc.psum_tensor("ps%d" % i, [128, 512], F32)) for i in range(8)]
        psb = [p[:].bitcast(BF16) for p in ps]

        def vcol(name, i=0):
            c = VP_COLS[name] + i
            return vp[:, c:c + 1]

        def act(out, in_, func, r, w, **kw):
            S.i("act", "activation", r, w, out=out, in_=in_, func=func, **kw)

        def mm(out, lhsT, rhs, r, w, start=True, stop=True, **kw):
            S.i("pe", "matmul", r, w, out=out, lhsT=lhsT, rhs=rhs, start=start, stop=stop, **kw)

        def tr(out, in_, ident, r, w):
            S.i("pe", "transpose", r, w, out=out, in_=in_, identity=ident)

        def tt(eng, out, in0, in1, op, r, w):
            S.i(eng, "tensor_tensor", r, w, out=out, in0=in0, in1=in1, op=op)

        def ts(eng, out, in0, s1, s2, op0, op1, r, w):
            if s2 is None:
                S.i(eng, "tensor_scalar", r, w, out=out, in0=in0, scalar1=s1, scalar2=None, op0=op0)
            else:
                S.i(eng, "tensor_scalar", r, w, out=out, in0=in0, scalar1=s1, scalar2=s2, op0=op0, op1=op1)

        def stt(out, in0, scalar, in1, op0, op1, r, w):
            S.i("dve", "scalar_tensor_tensor", r, w, out=out, in0=in0, scalar=scalar, in1=in1, op0=op0, op1=op1)

        def cp(eng, out, in_, r, w):
            if eng == "act":
                act(out, in_, AF.Copy, r, w)
            else:
                S.i(eng, "tensor_copy", r, w, out=out, in_=in_)

        def recip(out, in_, r, w):
            S.i("dve", "reciprocal", r, w, out=out, in_=in_)

        def sigmoid_from_exp(buf, tok, out=None, otok=None):
            act(buf, buf, AF.Ln, [tok], [tok], bias=1.0)
            if out is None:
                act(buf, buf, AF.Exp, [tok], [tok], scale=-1.0)
            else:
                act(out, buf, AF.Exp, [tok], [otok], scale=-1.0)

        def rsqrt_chain(dst, src, rtoks, wtok, scale, eps):
            act(dst, src, AF.Ln, rtoks, [wtok], scale=scale, bias=eps)
            act(dst, dst, AF.Exp, [wtok], [wtok], scale=-0.5)

        def dbg_store(name, src_ap, reads):
            if name in DBG:
                S.dma("pool", DBG[name], src_ap, reads=reads, semkey="dbg_" + name, store=True)

        for b_ in range(8):
            S.i("dve", "memset", [], ["ps%d" % b_], ap=ps[b_][:], constant=0.0)
        S.dma("sp", vp[:, 0:NVP_IN], I["vp"][:, :], writes=["vp"], semkey="c_vp")
        S.dma("sp", vbb[:], I["vb"].partition_broadcast(128), writes=["vbb"], semkey="c_vbb")
        S.dma("sp", identf[:], I["ident"][:, :], writes=["identf"], semkey="c_idf")
        S.dma("sp", rmask[64][:], I["rmask64"].partition_broadcast(128), writes=["rmask64"], semkey="c_rm64")
        S.dma("sp", rmask[32][:], I["rmask32"].partition_broadcast(128), writes=["rmask32"], semkey="c_rm32")
        S.dma("pool", identb[:], I["ident"][:, :], writes=["identb"], semkey="c_idb")
        S.dma("pool", e64[:], I["e64"][:, :], writes=["e64"], semkey="c_e64")
        S.dma("pool", ehead[:], I["ehead"][:, :], writes=["ehead"], semkey="c_eh")
        S.dma("pool", wlo[0:64, 0, :], I["w_b"][:, :], writes=["wlo#0"], semkey="c_wlo0")
        S.dma("pool", wlo[64:128, 0, :], I["a_b"][:, :], writes=["wlo#1"], semkey="c_wlo1")
        S.dma("pool", wlo[:, 1, :], I["g_b"][:, :], writes=["wlo#2"], semkey="c_wlo2")
        for k in masks:
            S.dma("pool", mk[k][:], I["m_" + k][:, :], writes=["m_" + k], semkey="c_m_" + k)

        def vps(name):
            c = VP_COLS[name]
            return vp[:, c:c + 4]
        tt("dve", vps("lb"), vps("lb1"), vps("lb0"), ALU.subtract, ["vp"], ["vp#lb"])
        act(vps("lb"), vps("lb"), AF.Exp, ["vp#lb"], ["vp#lb"])
        ts("dve", vps("lb"), vps("lb"), 1.0, None, ALU.add, None, ["vp#lb"], ["vp#lb"])
        recip(vps("lb"), vps("lb"), ["vp#lb"], ["vp#lb"])
        ts("dve", vps("oml"), vps("lb"), -1.0, 1.0, ALU.mult, ALU.add, ["vp#lb"], ["vp#oml"])
        ts("dve", vps("noml"), vps("lb"), 1.0, -1.0, ALU.mult, ALU.add, ["vp#lb"], ["vp#noml"])
        ts("dve", vps("nw0"), vps("w0"), -1.0, None, ALU.mult, None, ["vp"], ["vp#nw0"])
        ts("dve", vps("na0"), vps("a0"), -1.0, None, ALU.mult, None, ["vp"], ["vp#na0"])
        ts("dve", vps("omka"), vps("ka"), -1.0, 1.0, ALU.mult, ALU.add, ["vp"], ["vp#omka"])
        VPR = ["vp"]

        wstate = {"n": 0}
        wscr = {}

        def wload(wname, k0, nk, c0, ncols):
            wap = I[wname]
            if wname not in wscr:
                wscr[wname] = (nc.dram_tensor("scr_" + wname, list(wap.shape), BF16, kind="Internal").ap(), set())
            scr, done = wscr[wname]
            key = (k0, nk, c0, ncols)
            ctok = "scr_%s_%d_%d" % (wname, k0, c0)
            if key not in done:
                done.add(key)
                S.dma("pool", scr[k0 * 128:(k0 + nk) * 128, c0:c0 + ncols], wap[k0 * 128:(k0 + nk) * 128, c0:c0 + ncols],
                      writes=[ctok], semkey=ctok)
            i = wstate["n"] % NWS
            wstate["n"] += 1
            src = scr[k0 * 128:(k0 + nk) * 128, c0:c0 + ncols].rearrange("(kc p) c -> p kc c", p=128)
            S.dma("sp", wsl[i][:, 0:nk, 0:ncols], src, reads=[ctok], writes=["wsl%d" % i], semkey="wsl%d" % i)
            return wsl[i], "wsl%d" % i

        mmrot = {"n": 0}

        def mmbank():
            b = 4 + (mmrot["n"] % 4)
            mmrot["n"] += 1
            return b

        def rms_stats(xb, xtok, NS, TP):
            for s in range(NS):
                act(mixtm[0:TP, s, :], xb[0:TP, s, :], AF.Square, [xtok + "#%d" % s], ["mixtm#h%d" % s, "mixtm#r%d" % s, "stat#ss%d" % s],
                    accum_out=stat[0:TP, s:s + 1])
            rsqrt_chain(stat[0:TP, 4:4 + NS], stat[0:TP, 0:NS], ["stat#ss%d" % s for s in range(NS)], "stat#rs",
                        1.0 / D, RMS_EPS)

        def rms_to_fm(xb, xtok, NS, TP, gname, dst, dtok):
            for _ in rms_to_fm_gen(xb, xtok, NS, TP, gname, dst, dtok):
                pass

        def rms_to_fm_gen(xb, xtok, NS, TP, gname, dst, dtok):
            T = NS * TP
            rms_stats(xb, xtok, NS, TP)
            for s in range(min(NS, 2)):
                ts("dve", ntm[s % 2][0:TP, :], xb[0:TP, s, :], stat[0:TP, 4 + s:5 + s], None, ALU.mult, None,
                   [xtok + "#%d" % s, "stat#rs"], ["ntm%d" % (s % 2)])
            yield
            for s in range(NS):
                nb = ntm[s % 2]
                ntk = "ntm%d" % (s % 2)
                if s >= 2:
                    ts("dve", nb[0:TP, :], xb[0:TP, s, :], stat[0:TP, 4 + s:5 + s], None, ALU.mult, None,
                       [xtok + "#%d" % s, "stat#rs"], [ntk])
                for kc in range(8):
                    b = kc // 2
                    o = (kc % 2) * 512 + s * TP
                    tr(psb[b][:, o:o + TP], nb[0:TP, kc * 128:(kc + 1) * 128], identb[0:TP, 0:TP],
                       [ntk, "identb"], ["ps%d#t%d" % (b, kc % 2)])
            for kc in range(8):
                b = kc // 2
                o = (kc % 2) * 512
                if b % 2 == 0:
                    act(dst[:, kc, 0:T], psb[b][:, o:o + T], AF.Identity, ["ps%d#t%d" % (b, kc % 2)] + VPR, [dtok],
                        scale=vcol(gname, kc))
                else:
                    ts("dve", dst[:, kc, 0:T], psb[b][:, o:o + T], vcol(gname, kc), None, ALU.mult, None,
                       ["ps%d#t%d" % (b, kc % 2)] + VPR, [dtok])

        def tm_to_fm(srcf, stok, NS, TP, dst, dtok):
            T = NS * TP
            for s in range(NS):
                for kc in range(8):
                    b = kc // 2
                    o = (kc % 2) * 512 + s * TP
                    tr(psb[b][:, o:o + TP], srcf(s)[:, kc * 128:(kc + 1) * 128], identb[0:TP, 0:TP],
                       stok(s) + ["identb"], ["ps%d#t%d" % (b, kc % 2)])
            for b in range(4):
                if b % 2 == 0:
                    act(dst[:, 2 * b:2 * b + 2, 0:T], psb[b][:, :].rearrange("p (a t) -> p a t", a=2)[:, :, 0:T], AF.Copy,
                        ["ps%d" % b], [dtok])
                else:
                    cp("dve", dst[:, 2 * b:2 * b + 2, 0:T], psb[b][:, :].rearrange("p (a t) -> p a t", a=2)[:, :, 0:T],
                       ["ps%d" % b], [dtok])

        def proj_fm(src, stok, wt, wtok, ncols, T, evac, nk=8):
            for m in range(ncols // 128):
                b = mmbank()
                for kc in range(nk):
                    mm(ps[b][:, 0:T], wt[:, kc, m * 128:(m + 1) * 128], src[:, kc, 0:T], [stok, wtok], ["ps%d" % b],
                       start=(kc == 0), stop=(kc == nk - 1))
                evac(m, ps[b][:, 0:T], "ps%d" % b)

        def proj_tm(src, stok, wt, wtok, ncols, NS, TP, evac, nk=8):
            for s in range(NS):
                b = mmbank()
                for kc in range(nk):
                    mm(ps[b][0:TP, 0:ncols], src[:, kc, s * TP:(s + 1) * TP], wt[:, kc, 0:ncols], [stok, wtok], ["ps%d" % b],
                       start=(kc == 0), stop=(kc == nk - 1))
                evac(s, ps[b][0:TP, 0:ncols], "ps%d" % b)

        def resid_proj(cfg, src, stok, wname):
            NS, TP = cfg["NS"], cfg["TP"]
            xb, xtok = cfg["xb"], cfg["xtok"]
            for half in range(2):
                wt, wk = wload(wname, 0, 8, half * 512, 512)

                def ev(s, pv, ptok, half=half):
                    dst = xb[0:TP, s, half * 512:(half + 1) * 512]
                    tt("dve", dst, pv, dst, ALU.add, [ptok, xtok + "#%d" % s], [xtok + "#%d" % s])
                proj_tm(src, stok, wt, wk, 512, NS, TP, ev)

        def hgrn_mixer(cfg, hq, hE, hi, hsg):
            T, NS, TP, C = cfg["T"], cfg["NS"], cfg["TP"], cfg["C"]
            NCH = T // C
            CS = TP // C
            qg = aalloc("qg", (4, T), BF16)
            kg = aalloc("kg", (4, T), BF16)
            kd_off = bump[0]
            kd = aalloc("kd", (4, T), BF16)
            kdtm = aalloc("kdtm", (NS, 512), BF16)
            otmp = [None, None]
            ebl = aalloc("ebl", (4, NCH), F32)
            hst = aalloc("hst", (1, 16), F32)
            tmps_off = bump[0]
            tmps = [[aalloc("h%s%d" % (n, i), (1, T), F32) for n in ("sg", "lf", "kf", "e1", "e2")] for i in range(1)]
            tmps = tmps + tmps
            end_off = bump[0]
            if T == 512:
                bump[0] = kd_off
                attm = [aalloc("attm%d" % i, (4, 128), BF16) for i in range(2)]
                otmp[0] = aalloc("otmp0", (4, 128), F32)
                otmp[1] = otmp[0]
                assert bump[0] <= kd_off + 4 * T * 2
                bump[0] = end_off
                attm = attm + [aalloc("attm%d" % i, (4, 128), BF16) for i in range(2, NS)]
                otmp[1] = otmp[0]
            else:
                attm = [aalloc("attm%d" % i, (4, 128), BF16) for i in range(NS)]
                otmp[0] = aalloc("otmp0", (4, 128), F32)
                otmp[1] = otmp[0]
            rm = rmask[C]
            if T == 512:
                bump_keep = bump[0]
                bump[0] = tmps_off
                dg = [aalloc("hdg%d" % i, (4, 128), F32) for i in range(2)]
                assert bump[0] <= end_off
                bump[0] = bump_keep
            else:
                dg = [aalloc("hdg%d" % i, (4, 128), F32) for i in range(2)]
            yield
            for h in range(4):
                S.phase = "HGprep"
                sg, lf, kf, e1, e2 = [t[:, 0, :] for t in tmps[h % 2]]
                bb = lf
                k_ = lambda n, h=h: "h%s%d" % (n, 0)
                sigmoid_from_exp(hE[:, h, :], "hE#%d" % h, out=sg, otok=k_("sg"))
                act(lf, sg, AF.Ln, [k_("sg")] + VPR, [k_("lf")], scale=vcol("oml", h), bias=vcol("lb", h))
                ts("pool", kf, sg, vcol("noml", h), vcol("oml", h), ALU.mult, ALU.add, [k_("sg")] + VPR, [k_("kf")])
                S.i("dve", "tensor_tensor_scan", [k_("lf"), "rmask%d" % C], [k_("lf")], out=bb, data0=rm[:, 0:T], data1=lf,
                    initial=0.0, op0=ALU.mult, op1=ALU.add)
                act(e1, bb, AF.Exp, [k_("lf")], [k_("e1")])
                act(e2, bb, AF.Exp, [k_("lf")], [k_("e2")], scale=-1.0)
                tt("dve", qg[:, h, :], hq[:, h, :], e1, ALU.mult, ["hq#%d" % h, k_("e1")], ["qg#%d" % h])
                tt("pool", kf, kf, e2, ALU.mult, [k_("kf"), k_("e2")], [k_("kf")])
                cp("pool", kg[:, h, :], kf, [k_("kf")], ["kg#%d" % h])
                cp("dve", ebl[:, h, :], e1.rearrange("p (n c) -> p n c", c=C)[:, :, C - 1], [k_("e1")], ["ebl#%d" % h])
                tt("dve", kd[:, h, :].rearrange("p (n c) -> p n c", c=C), kf.rearrange("p (n c) -> p n c", c=C),
                   ebl[:, h, :].unsqueeze(2).to_broadcast([128, NCH, C]), ALU.mult, [k_("kf"), "ebl#%d" % h], ["kd#%d" % h])
                yield
            S.phase = "HG"
            for s in range(NS):
                b = 6 + (s % 2)
                for h in range(4):
                    tr(psb[b][0:TP, h * 128:(h + 1) * 128], kd[:, h, s * TP:(s + 1) * TP], identb[:, :],
                       ["kd#%d" % h, "identb"], ["ps%d" % b])
                cp("act", kdtm[0:TP, s, :], psb[b][0:TP, 0:512], ["ps%d" % b], ["kdtm#%d" % s])
            mname = "hg%d" % C
            for s in range(NS):
                bA, bO = 4 + (s % 2), s
                am, amk = attm[s], "attm%d" % s
                tsl = slice(s * TP, (s + 1) * TP)
                for h in range(4):
                    mm(ps[bA][0:TP, h * 128:h * 128 + TP], kg[:, h, tsl], qg[:, h, tsl], ["kg#%d" % h, "qg#%d" % h],
                       ["ps%d" % bA])
                tt("dve", am[0:TP, :, 0:TP], ps[bA][0:TP, :].rearrange("p (h t) -> p h t", h=4)[:, :, 0:TP],
                   mk[mname][0:TP, 0:TP].unsqueeze(1).to_broadcast([TP, 4, TP]), ALU.mult, ["ps%d" % bA, "m_" + mname], [amk])
                for h in range(4):
                    hc = slice(h * 128, (h + 1) * 128)
                    mm(ps[bO][0:TP, hc], am[0:TP, h, 0:TP], hi[0:TP, s, hc], [amk, "hi#%d" % s], ["ps%d" % bO],
                       start=(h == 0), stop=False, skip_group_check=True)
            for s in range(NS):
                bO = s
                for j in range(CS):
                    ch = s * CS + j
                    Sf, Sb, Stok = cfg["S_hg"](ch)
                    rows = slice(j * C, (j + 1) * C)
                    tch = slice(ch * C, (ch + 1) * C)
                    bU = 6 + (ch % 2)
                    for h in range(4):
                        hc = slice(h * 128, (h + 1) * 128)
                        mm(ps[bO][rows, hc], qg[:, h, tch], Sb[:, h, :], ["qg#%d" % h, Stok + "b#%d" % h], ["ps%d" % bO],
                           start=False, stop=(j == CS - 1), skip_group_check=True)
                    dgs = dg[ch % 2]
                    for h in range(4):
                        ts("pool", dgs[:, h, :], identf[:, :], ebl[:, h, ch:ch + 1], 1.0, ALU.mult, ALU.mult,
                           ["identf", "ebl#%d" % h], ["hdg%d#%d" % (ch % 2, h)])
                    for h in range(4):
                        hc = slice(h * 128, (h + 1) * 128)
                        mm(ps[bU][:, hc], kdtm[rows, s, hc], hi[rows, s, hc], ["kdtm#%d" % s, "hi#%d" % s], ["ps%d" % bU],
                           start=True, stop=False, skip_group_check=True)
                        mm(ps[bU][:, hc], dgs[:, h, :], Sf[:, h, :], ["hdg%d#%d" % (ch % 2, h), Stok + "f#%d" % h], ["ps%d" % bU],
                           start=False, stop=True, skip_group_check=True)
                    act(Sb[:, :, :].rearrange("p h v -> p (h v)"), ps[bU][:, 0:512], AF.Copy, ["ps%d" % bU],
                        [Stok + "b#%d" % h for h in range(4)])
                    cp("dve", Sf[:, :, :].rearrange("p h v -> p (h v)"), ps[bU][:, 0:512], ["ps%d" % bU],
                       [Stok + "f#%d" % h for h in range(4)])
            for s in range(NS):
                bO = s
                ot, otk = otmp[0], "otmp0"
                for h in range(4):
                    act(ot[0:TP, h, :], ps[bO][0:TP, h * 128:(h + 1) * 128], AF.Square, ["ps%d" % bO], [otk, "hst#s%d" % h],
                        accum_out=hst[0:TP, 0, h:h + 1])
                rsqrt_chain(hst[0:TP, 0, 4:8], hst[0:TP, 0, 0:4], ["hst#s%d" % h for h in range(4)], "hst#r", 1.0 / 128, RMS_EPS)
                tt("dve", ot[0:TP, :, :], ps[bO][0:TP, :].rearrange("p (h v) -> p h v", h=4),
                   hst[0:TP, 0, 4:8].unsqueeze(2).to_broadcast([TP, 4, 128]), ALU.mult, ["ps%d" % bO, "hst#r", otk], [otk])
                tt("pool", mixtm[0:TP, s, 0:512], ot[0:TP, :, :].rearrange("p h v -> p (h v)"), hsg[0:TP, s, :], ALU.mult,
                   [otk, "hsg#%d" % s], ["mixtm#h%d" % s])
            yield

        def rwkv_mixer(cfg, rkv, lora, rwg):
            T, NS, TP, C = cfg["T"], cfg["NS"], cfg["TP"], cfg["C"]
            NCH = T // C
            CS = TP // C
            HC = 2 * C
            L = {64: 5, 32: 4}[C]
            bump[0] = cfg["mark1"]
            tha = aalloc("tha", (1, T), BF16)
            sgd = aalloc("sgd", (1, T), BF16)
            sqbs = [aalloc("sqb%d" % q, (1, T), BF16) for q in range(2)]
            rkbs = [aalloc("rkb%d" % q, (1, T), BF16) for q in range(2)]
            ysq_ = aalloc("ysq_", (1, 512), F32)
            cen_ = aalloc("cen_", (1, 512), F32)
            wc = aalloc("wc", (4, NCH), F32)
            bon = aalloc("bon", (NS, 8), F32)
            gst = aalloc("gst", (1, 32), F32)
            yrw = aalloc("yrw", (NS, 512), F32)
            kt_ = aalloc("kt_", (4, T), BF16)
            rt_ = aalloc("rt_", (4, T), BF16)
            bt_ = aalloc("bt_", (4, T), BF16)
            ktl = aalloc("ktl", (4, T), BF16)
            ft_off = bump[0]
            ftset = [[aalloc("rt%d_%d" % (q, i), (1, T), F32)[:, 0, :] for i in range(6)] for q in range(2)]
            fkset = [["rt%d_%d" % (q, i) for i in range(6)] for q in range(2)]
            ft, fk = ftset[0], fkset[0]
            NSET = 4
            cs_ = []
            end_rw = bump[0]
            bump[0] = ft_off
            for i in range(NSET):
                d = dict(tm=aalloc("c%dtm" % i, (5, 128), BF16), N=aalloc("c%dN" % i, (1, 128), BF16),
                         NTs=aalloc("c%dNTs" % i, (1, 256), BF16), AkT=aalloc("c%dAkT" % i, (1, 256), BF16),
                         X=[aalloc("c%dX%d" % (i, j), (2, 128), BF16) for j in range(2)],
                         TT=[aalloc("c%dTT%d" % (i, j), (1, 128), BF16) for j in range(2)],
                         WU=aalloc("c%dWU" % i, (2, 128), BF16), Mp=aalloc("c%dMp" % i, (1, 128), BF16),
                         GT=aalloc("c%dGT" % i, (1, 64), BF16))
                cs_.append(d)
            bump[0] = max(bump[0], end_rw)
            rm = rmask[C]
            t0 = ft[0]
            act(t0[0:64, :], lora[0:64, 0, :], AF.Exp, ["lora#0"], [fk[0]], scale=2.0)
            sigmoid_from_exp(t0[0:64, :], fk[0])
            ts("dve", tha[0:64, 0, :], t0[0:64, :], -2.0, 1.0, ALU.mult, ALU.add, [fk[0]], ["tha#0"])
            cp("pool", tha[64:128, 0, :], lora[64:128, 0, :], ["lora#0"], ["tha#1"])
            t1 = ft[1]
            act(t1, lora[:, 1, :], AF.Exp, ["lora#1"], [fk[1]], scale=-1.0)
            sigmoid_from_exp(t1, fk[1], out=sgd[:, 0, :], otok="sgd")
            for s in range(NS):
                b = mmbank()
                mm(ps[b][0:TP, :], sgd[:, 0, s * TP:(s + 1) * TP], wlo[:, 1, :], ["sgd", "wlo#2"], ["ps%d" % b])
                cp("act", rwg[0:TP, s, :], ps[b][0:TP, :], ["ps%d" % b], ["rwg#%d" % s])
            def prep_gen(hp):
                hpc = slice(hp * 128, (hp + 1) * 128)
                xr, xk = rkv[:, hp, :], rkv[:, 4 + hp, :]
                rtok, ktok = "rkv#%d" % hp, "rkv#%d" % (4 + hp)
                sz, a_, kk, rn, k2, ex = ftset[hp % 2]
                ksz, ka_, kkk, krn, kk2, kex = fkset[hp % 2]
                cs, kcs = a_, ka_
                sqb, rkb = sqbs[hp % 2], rkbs[hp % 2]
                sqk, rkk = "sqb%d" % (hp % 2), "rkb%d" % (hp % 2)
                b = mmbank()
                mm(ps[b][:, 0:T], wlo[0:64, 0, hpc], tha[0:64, 0, :], ["wlo#0", "tha#0"], ["ps%d" % b])
                act(sz, ps[b][:, 0:T], AF.Exp, ["ps%d" % b] + VPR, [ksz], scale=-1.0, bias=vcol("nw0", hp))
                sigmoid_from_exp(sz, ksz)
                yield
                b = mmbank()
                mm(ps[b][:, 0:T], wlo[64:128, 0, hpc], tha[64:128, 0, :], ["wlo#1", "tha#1"], ["ps%d" % b])
                act(a_, ps[b][:, 0:T], AF.Exp, ["ps%d" % b] + VPR, [ka_], scale=-1.0, bias=vcol("na0", hp))
                sigmoid_from_exp(a_, ka_)
                yield
                ts("dve", kk, xk, vcol("kk", hp), None, ALU.mult, None, [ktok] + VPR, [kkk])
                act(sqb[:, 0, :], kk, AF.Square, [kkk], [sqk])
                yield
                b = mmbank()
                mm(ps[b][:, 0:T], e64[:, :], sqb[:, 0, :], ["e64", sqk], ["ps%d" % b])
                ts("dve", rn, ps[b][:, 0:T], 1e-24, None, ALU.max, None, ["ps%d" % b], [krn])
                act(rn, rn, AF.Ln, [krn], [krn])
                act(rn, rn, AF.Exp, [krn], [krn], scale=-0.5)
                yield
                tt("dve", kk, kk, rn, ALU.mult, [kkk, krn], [kkk])
                ts("pool", k2, a_, vcol("ka", hp), vcol("omka", hp), ALU.mult, ALU.add, [ka_] + VPR, [kk2])
                tt("dve", k2, k2, xk, ALU.mult, [kk2, ktok], [kk2])
                yield
                stt(rkb[:, 0, :], k2, vcol("rk", hp), xr, ALU.mult, ALU.mult, [kk2, rtok] + VPR, [rkk])
                tt("pool", rn, kk, a_, ALU.mult, [kkk, ka_], [krn])
                yield
                S.i("dve", "tensor_tensor_scan", [ksz, "rmask%d" % C], [kcs], out=cs, data0=rm[:, 0:T], data1=sz,
                    initial=0.0, op0=ALU.mult, op1=ALU.add)
                act(ex, cs, AF.Exp, [kcs], [kex], scale=-DECAY_K)
                yield
                cp("act", wc[:, hp, :], ex.rearrange("p (n c) -> p n c", c=C)[:, :, C - 1], [kex], ["wc#%d" % hp])
                tt("dve", rt_[:, hp, :], xr, ex, ALU.mult, [rtok, kex], ["rt_#%d" % hp])
                act(ex, cs, AF.Exp, [kcs], [kex], scale=DECAY_K)
                yield
                tt("dve", bt_[:, hp, :], rn, ex, ALU.mult, [krn, kex], ["bt_#%d" % hp])
                tt("pool", ktl[:, hp, :], k2, ex, ALU.mult, [kk2, kex], ["ktl#%d" % hp])
                tt("dve", sz, cs, sz, ALU.subtract, [kcs, ksz], [ksz])
                yield
                act(ex, sz, AF.Exp, [ksz], [kex], scale=-DECAY_K)
                tt("dve", kt_[:, hp, :], kk, ex, ALU.mult, [kkk, kex], ["kt_#%d" % hp])
                yield
                for s in range(NS):
                    bb_ = mmbank()
                    mm(ps[bb_][0:TP, 0:2], rkb[:, 0, s * TP:(s + 1) * TP], ehead[:, :], [rkk, "ehead"], ["ps%d" % bb_])
                    cp("act", bon[0:TP, s, 2 * hp:2 * hp + 2], ps[bb_][0:TP, 0:2], ["ps%d" % bb_], ["bon#%d_%d" % (s, hp)])

            for pair in ((0, 1), (2, 3)):
                gens = [prep_gen(h_) for h_ in pair]
                while gens:
                    for g_ in list(gens):
                        try:
                            next(g_)
                        except StopIteration:
                            gens.remove(g_)
            dbg_store("kt_", kt_[:, :, :], ["kt_"])
            dbg_store("rt_", rt_[:, :, :], ["rt_"])
            dbg_store("bt_", bt_[:, :, :], ["bt_"])
            dbg_store("ktl", ktl[:, :, :], ["ktl"])
            dbg_store("wc", wc[:, :, :], ["wc"])
            o_w, o_b = VB_COLS["gnw"], VB_COLS["gnb"]
            ysq, cen = ysq_[:, 0, :], cen_[:, 0, :]
            fk = ["ysq_", "cen_"]
            def rw_post(s):
                S.phase = "RWpost"
                y3 = yrw[0:TP, s, :].rearrange("p (h v) -> p h v", h=8)
                yk = "yrw#%d" % s
                S.i("dve", "tensor_reduce", [yk], ["gst#s1"], out=gst[0:TP, 0, 0:8], in_=y3, axis=mybir.AxisListType.X,
                    op=ALU.add)
                act(ysq[0:TP, 0:512], yrw[0:TP, s, :], AF.Square, [yk], [fk[0]])
                S.i("dve", "tensor_reduce", [fk[0]], ["gst#s2"], out=gst[0:TP, 0, 8:16],
                    in_=ysq[0:TP, 0:512].rearrange("p (h v) -> p h v", h=8), axis=mybir.AxisListType.X, op=ALU.add)
                ts("dve", gst[0:TP, 0, 0:8], gst[0:TP, 0, 0:8], 1.0 / 64, None, ALU.mult, None, ["gst#s1"], ["gst#s1"])
                tt("dve", gst[0:TP, 0, 16:24], gst[0:TP, 0, 0:8], gst[0:TP, 0, 0:8], ALU.mult, ["gst#s1"], ["gst#m2"])
                stt(gst[0:TP, 0, 8:16], gst[0:TP, 0, 8:16], 1.0 / 64, gst[0:TP, 0, 16:24], ALU.mult, ALU.subtract,
                    ["gst#s2", "gst#m2"], ["gst#s2"])
                rsqrt_chain(gst[0:TP, 0, 8:16], gst[0:TP, 0, 8:16], ["gst#s2"], "gst#s2", 1.0, GN_EPS)
                c3 = cen[0:TP, 0:512].rearrange("p (h v) -> p h v", h=8)
                tt("dve", c3, y3, gst[0:TP, 0, 0:8].unsqueeze(2).to_broadcast([TP, 8, 64]), ALU.subtract, [yk, "gst#s1"], [fk[1]])
                tt("dve", c3, c3, gst[0:TP, 0, 8:16].unsqueeze(2).to_broadcast([TP, 8, 64]), ALU.mult, [fk[1], "gst#s2"], [fk[1]])
                tt("pool", cen[0:TP, 0:512], cen[0:TP, 0:512], vbb[0:TP, o_w:o_w + 512], ALU.mult, [fk[1], "vbb"], [fk[1]])
                tt("pool", cen[0:TP, 0:512], cen[0:TP, 0:512], vbb[0:TP, o_b:o_b + 512], ALU.add, [fk[1], "vbb"], [fk[1]])
                bV = mmbank()
                for hp in range(4):
                    tr(psb[bV][0:TP, hp * 128:(hp + 1) * 128], rkv[:, 8 + hp, s * TP:(s + 1) * TP], identb[:, :],
                       ["rkv#%d" % (8 + hp), "identb"], ["ps%d" % bV])
                tt("dve", ysq[0:TP, 0:512].rearrange("p (h v) -> p h v", h=8),
                   psb[bV][0:TP, 0:512].rearrange("p (h v) -> p h v", h=8),
                   bon[0:TP, s, :].unsqueeze(2).to_broadcast([TP, 8, 64]), ALU.mult,
                   ["ps%d" % bV] + ["bon#%d_%d" % (s, hp) for hp in range(4)], [fk[0]])
                tt("pool", cen[0:TP, 0:512], cen[0:TP, 0:512], ysq[0:TP, 0:512], ALU.add, [fk[0], fk[1]], [fk[1]])
                tt("dve", mixtm[0:TP, s, 512:1024], cen[0:TP, 0:512], rwg[0:TP, s, :], ALU.mult, [fk[1], "rwg#%d" % s],
                   ["mixtm#r%d" % s])


            S.phase = "RWchunk"
            for i in range(NSET):
                for nm in ("tm",):
                    S.i("pool", "memset", [], ["c%d%s" % (i, nm)], ap=cs_[i][nm][:], constant=0.0)
            hr = lambda h: slice(h * 64, (h + 1) * 64)
            trw = lambda h: slice(h * C, (h + 1) * C)
            m1, m2, m3 = mk["a1_%d" % C], mk["a2_%d" % C], mk["a3_%d" % C]
            for ch in range(NCH):
                csl = slice(ch * C, (ch + 1) * C)
                s, j = ch // CS, ch % CS
                Hf, Hb, Htok = cfg["H_rw"](ch)
                st = [cs_[hp % NSET] for hp in range(4)]
                pre = ["c%d" % (hp % NSET) for hp in range(4)]
                for hp in range(4):
                    d, p_ = st[hp], pre[hp]
                    bA, bB = 2 * hp, 2 * hp + 1
                    arrs = [(kt_[:, hp, :], "kt_#%d" % hp), (rkv[:, 8 + hp, :], "rkv#%d" % (8 + hp)),
                            (bt_[:, hp, :], "bt_#%d" % hp), (ktl[:, hp, :], "ktl#%d" % hp)]
                    for a, (arr, atok) in enumerate(arrs):
                        for h in range(2):
                            tr(psb[bA][trw(h), a * 128 + h * 64:a * 128 + (h + 1) * 64], arr[hr(h), csl],
                               identb[hr(h), hr(h)], [atok, "identb"], ["ps%d" % bA])
                    for h in range(2):
                        src = psb[bA][trw(h), 0:512].rearrange("p (a c) -> p a c", a=4)[:, :, h * 64:(h + 1) * 64]
                        dst = d["tm"][trw(h), :, :]
                        eng = "act" if h == 0 else "dve"
                        cp(eng, dst[:, 0:1, h * 64:(h + 1) * 64], src[:, 0:1, :], ["ps%d" % bA], [p_ + "tm#a"])
                        cp(eng, dst[:, 2:5, h * 64:(h + 1) * 64], src[:, 1:4, :], ["ps%d" % bA], [p_ + "tm#b"])
                    for h in range(2):
                        kth, bth = kt_[hr(h), hp, csl], bt_[hr(h), hp, csl]
                        rth, klh = rt_[hr(h), hp, csl], ktl[hr(h), hp, csl]
                        rw_ = ["kt_#%d" % hp, "bt_#%d" % hp, "rt_#%d" % hp, "ktl#%d" % hp]
                        mm(ps[bB][trw(h), h * C:(h + 1) * C], kth, bth, rw_, ["ps%d#m1" % bB])
                        mm(ps[bB][trw(h), 128 + h * C:128 + (h + 1) * C], bth, kth, rw_, ["ps%d#m2" % bB])
                        mm(ps[bB][trw(h), 128 + 2 * C:128 + 3 * C], bth, rth, rw_, ["ps%d#m2" % bB])
                        mm(ps[bB][trw(h), 320 + h * C:320 + (h + 1) * C], klh, kth, rw_, ["ps%d#m3" % bB])
                        mm(ps[bB][trw(h), 320 + 2 * C:320 + 3 * C], klh, rth, rw_, ["ps%d#m3" % bB])
                    tt("dve", d["N"][0:HC, 0, 0:HC], ps[bB][0:HC, 0:HC], m1[0:HC, 0:HC], ALU.mult,
                       ["ps%d#m1" % bB, "m_a1_%d" % C], [p_ + "N"])
                    tt("dve", d["NTs"][0:HC, 0, 0:3 * C], ps[bB][0:HC, 128:128 + 3 * C], m2[0:HC, 0:3 * C], ALU.mult,
                       ["ps%d#m2" % bB, "m_a2_%d" % C], [p_ + "NTs"])
                    tt("dve", d["AkT"][0:HC, 0, 0:3 * C], ps[bB][0:HC, 320:320 + 3 * C], m3[0:HC, 0:3 * C], ALU.mult,
                       ["ps%d#m3" % bB, "m_a3_%d" % C], [p_ + "AkT"])
                for hp in range(4):
                    d, p_ = st[hp], pre[hp]
                    bA = 2 * hp
                    mm(ps[bA][0:HC, 0:128], d["AkT"][0:HC, 0, 0:HC], d["tm"][0:HC, 2, :], [p_ + "AkT", p_ + "tm#b"], ["ps%d" % bA])
                    cp("act", d["tm"][0:HC, 1, :], ps[bA][0:HC, 0:128], ["ps%d" % bA], [p_ + "tm#c"])
                    tt("pool", d["TT"][0][0:HC, 0, 0:HC], d["NTs"][0:HC, 0, 0:HC], identb[0:HC, 0:HC], ALU.add,
                       [p_ + "NTs", "identb"], [p_ + "TT0"])
                for lv in range(1, L + 1):
                    for hp in range(4):
                        d, p_ = st[hp], pre[hp]
                        bX = 2 * hp + (lv % 2)
                        if lv == 1:
                            Xp, XTp = d["N"][0:HC, 0, 0:HC], d["NTs"][0:HC, 0, 0:HC]
                            rp = [p_ + "N", p_ + "NTs"]
                        else:
                            Xp, XTp = d["X"][(lv - 1) % 2][0:HC, 0, 0:HC], d["X"][(lv - 1) % 2][0:HC, 1, 0:HC]
                            rp = [p_ + "X%d" % ((lv - 1) % 2)]
                        Xn, xnk = d["X"][lv % 2], p_ + "X%d" % (lv % 2)
                        mm(ps[bX][0:HC, 0:HC], XTp, Xp, rp, ["ps%d" % bX])
                        if lv < L:
                            mm(ps[bX][0:HC, 128:128 + HC], Xp, XTp, rp, ["ps%d" % bX])
                            src = ps[bX][0:HC, 0:256].rearrange("p (a b) -> p a b", a=2)[:, :, 0:HC]
                            cp("act" if hp % 2 == 0 else "dve", Xn[0:HC, :, 0:HC], src, ["ps%d" % bX], [xnk])
                        else:
                            cp("act" if hp % 2 == 0 else "dve", Xn[0:HC, 0, 0:HC], ps[bX][0:HC, 0:HC], ["ps%d" % bX], [xnk])
                    for hp in range(4):
                        d, p_ = st[hp], pre[hp]
                        bT = 2 * hp + ((lv + 1) % 2)
                        Xn, xnk = d["X"][lv % 2], p_ + "X%d" % (lv % 2)
                        TTo, TTn = d["TT"][(lv - 1) % 2], d["TT"][lv % 2]
                        tko, tkn = p_ + "TT%d" % ((lv - 1) % 2), p_ + "TT%d" % (lv % 2)
                        mm(ps[bT][0:HC, 256:256 + HC], identb[0:HC, 0:HC], TTo[0:HC, 0, 0:HC], ["identb", tko], ["ps%d" % bT],
                           start=True, stop=False, skip_group_check=True)
                        mm(ps[bT][0:HC, 256:256 + HC], Xn[0:HC, 0, 0:HC], TTo[0:HC, 0, 0:HC], [xnk, tko], ["ps%d" % bT],
                           start=False, stop=True, skip_group_check=True)
                        cp("act" if hp % 2 == 1 else "dve", TTn[0:HC, 0, 0:HC], ps[bT][0:HC, 256:256 + HC], ["ps%d" % bT], [tkn])
                for hp in range(4):
                    d, p_ = st[hp], pre[hp]
                    bA = 2 * hp
                    TTL, tkl = d["TT"][L % 2], p_ + "TT%d" % (L % 2)
                    mm(ps[bA][0:HC, 0:256], TTL[0:HC, 0, 0:HC], d["tm"][0:HC, 0:2, :].rearrange("p a b -> p (a b)"),
                       [tkl, p_ + "tm#a", p_ + "tm#c"], ["ps%d" % bA])
                    act(d["WU"][0:HC, :, :].rearrange("p a b -> p (a b)"), ps[bA][0:HC, 0:256], AF.Copy, ["ps%d" % bA],
                        [p_ + "WU"], scale=-1.0)
                for hp in range(4):
                    d, p_ = st[hp], pre[hp]
                    bB = 2 * hp + 1
                    mm(ps[bB][:, 0:128], d["WU"][0:HC, 0, :], d["tm"][0:HC, 3, :], [p_ + "WU", p_ + "tm#b"], ["ps%d#m1" % bB])
                    mm(ps[bB][:, 128:128 + C], d["WU"][0:HC, 0, :], d["NTs"][0:HC, 0, 2 * C:3 * C], [p_ + "WU", p_ + "NTs"],
                       ["ps%d#m2" % bB])
                    tt("dve", d["Mp"][:, 0, :], ps[bB][:, 0:128], identb[:, :], ALU.add, ["ps%d#m1" % bB, "identb"], [p_ + "Mp"])
                    tt("dve", d["GT"][:, 0, 0:C], ps[bB][:, 128:128 + C], rt_[:, hp, csl], ALU.add,
                       ["ps%d#m2" % bB, "rt_#%d" % hp], [p_ + "GT"])
                for hp in range(4):
                    d, p_ = st[hp], pre[hp]
                    bA, bB = 2 * hp, 2 * hp + 1
                    hpc = slice(hp * 128, (hp + 1) * 128)
                    rows = slice(j * C, (j + 1) * C)
                    hbk, hfk = Htok + "b#%d" % hp, Htok + "f#%d" % hp
                    mm(ps[bA][rows, 256:384], d["NTs"][0:HC, 0, 2 * C:3 * C], d["WU"][0:HC, 1, :], [p_ + "NTs", p_ + "WU"],
                       ["ps%d" % bA], start=True, stop=False)
                    mm(ps[bA][rows, 256:384], d["AkT"][0:HC, 0, 2 * C:3 * C], d["tm"][0:HC, 2, :], [p_ + "AkT", p_ + "tm#b"],
                       ["ps%d" % bA], start=False, stop=False)
                    mm(ps[bA][rows, 256:384], d["GT"][:, 0, 0:C], Hb[:, hp, :], [p_ + "GT", hbk], ["ps%d" % bA],
                       start=False, stop=True)
                    cp("act", yrw[rows, s, hpc], ps[bA][rows, 256:384], ["ps%d" % bA], ["yrw#%d" % s])
                    mm(ps[bB][:, 320:448], d["tm"][0:HC, 3, :], d["WU"][0:HC, 1, :], [p_ + "tm#b", p_ + "WU"], ["ps%d#m3" % bB],
                       start=True, stop=False)
                    mm(ps[bB][:, 320:448], d["tm"][0:HC, 4, :], d["tm"][0:HC, 2, :], [p_ + "tm#b"], ["ps%d#m3" % bB],
                       start=False, stop=False)
                    mm(ps[bB][:, 320:448], d["Mp"][:, 0, :], Hb[:, hp, :], [p_ + "Mp", hbk], ["ps%d#m3" % bB],
                       start=False, stop=True)
                    if cfg["hf_chunks"] is None or ch in cfg["hf_chunks"]:
                        ts("dve", Hf[:, hp, :], ps[bB][:, 320:448], wc[:, hp, ch:ch + 1], None, ALU.mult, None,
                           ["ps%d#m3" % bB, "wc#%d" % hp], [hfk])
                    act(Hb[:, hp, :], ps[bB][:, 320:448], AF.Identity, ["ps%d#m3" % bB, "wc#%d" % hp], [hbk],
                        scale=wc[:, hp, ch:ch + 1])
                if j == CS - 1:
                    rw_post(s)
                    S.phase = "RWchunk"
            dbg_store("yrw", yrw[0:TP, :, :], ["yrw"])

        def cross_attn(cfg):
            T, NS, TP = cfg["T"], cfg["NS"], cfg["TP"]
            xb, xtok = cfg["xb"], cfg["xtok"]
            bump[0] = 0
            qfm = aalloc("qfm", (8, T), BF16)
            prT = aalloc("prT", (8, T), BF16)
            prf = [aalloc("prf%d" % i, (4, NMEM), F32) for i in range(2)]
            prb = [aalloc("prb%d" % i, (4, NMEM), BF16) for i in range(2)]
            cst = aalloc("cst", (2, 16), F32)
            if cfg["sample"]:
                smem = []
                for q in range(2):
                    mkf = aalloc("smk%d" % q, (8, NMEM), BF16)
                    mvt = aalloc("smv%d" % q, (2, D), BF16)
                    mkt = aalloc("smkt%d" % q, (2, D), BF16)
                    S.dma("pool", mvt[:, :, :], I["cv"][q].rearrange("(mc p) d -> p mc d", p=128), writes=["sm%dv" % q],
                          semkey="smv%d" % q)
                    S.dma("pool", mkt[:, :, :], I["ck"][q].rearrange("(mc p) d -> p mc d", p=128), writes=["smkt%d" % q],
                          semkey="smk%d" % q)
                    for dch in range(8):
                        b = mmbank()
                        for mc in range(2):
                            tr(psb[b][:, mc * 128:(mc + 1) * 128], mkt[:, mc, dch * 128:(dch + 1) * 128], identb[:, :],
                               ["smkt%d" % q, "identb"], ["ps%d" % b])
                        cp("act" if dch % 2 == 0 else "dve", mkf[:, dch, :], psb[b][:, 0:NMEM], ["ps%d" % b], ["sm%dk" % q])
                    smem.append((mkf, mvt, "sm%d" % q))
                cfg["mem"] = lambda mi: smem[mi]
            rms_to_fm(xb, xtok, NS, TP, "g_cross", fm[0], "fm0")
            for half in range(2):
                wt, wk = wload("w_cq", 0, 8, half * 512, 512)

                def evq(m, pv, ptok, half=half):
                    act(qfm[:, half * 4 + m, :], pv, AF.Copy, [ptok], ["qfm#%d" % (half * 4 + m)], scale=1.0 / 16.0)
                proj_fm(fm[0], "fm0", wt, wk, 512, T, evq)
            cstk = lambda s, n: "cst%d#%s" % (s % 2, n)

            def scores(s):
                tsl = slice(s * TP, (s + 1) * TP)
                b0 = 4 + 2 * (s % 2)
                for (rows, mi) in cfg["att_groups"](s):
                    mkfm_i, mvtm_i, mtok = cfg["mem"](mi)
                    for h in range(4):
                        b = b0 + (h // 2)
                        for dc in range(2):
                            mm(ps[b][rows, (h % 2) * 256:(h % 2) * 256 + NMEM],
                               qfm[:, 2 * h + dc, tsl.start + rows.start:tsl.start + rows.stop],
                               mkfm_i[:, 2 * h + dc, :], ["qfm#%d" % (2 * h + dc), mtok + "k"], ["ps%d" % b],
                               start=(dc == 0), stop=(dc == 1), skip_group_check=True)

                cs2 = cst[:, s % 2, :]
                for bi, hh in ((0, 0), (1, 2)):
                    b = b0 + bi
                    S.i("dve", "tensor_reduce", ["ps%d" % b], [cstk(s, "m%d" % bi)], out=cs2[0:TP, hh:hh + 2],
                        in_=ps[b][0:TP, :].rearrange("p (h m) -> p h m", h=2), axis=mybir.AxisListType.X, op=ALU.max)
                ts("dve", cs2[0:TP, 4:8], cs2[0:TP, 0:4], -1.0, None, ALU.mult, None, [cstk(s, "m0"), cstk(s, "m1")], [cstk(s, "n")])

            def softmax_T(s):
                tsl = slice(s * TP, (s + 1) * TP)
                b0 = 4 + 2 * (s % 2)
                pf_, pb_ = prf[s % 2], prb[s % 2]
                pfk, pbk = "prf%d" % (s % 2), "prb%d" % (s % 2)
                cs2 = cst[:, s % 2, :]
                for h in range(4):
                    b = b0 + (h // 2)
                    act(pf_[0:TP, h, :], ps[b][0:TP, (h % 2) * 256:(h % 2) * 256 + NMEM], AF.Exp, ["ps%d" % b, cstk(s, "n")],
                        [pfk + "#%d" % h, cstk(s, "s%d" % h)], bias=cs2[0:TP, 4 + h:5 + h], accum_out=cs2[0:TP, 8 + h:9 + h])
                recip(cs2[0:TP, 12:16], cs2[0:TP, 8:12], [cstk(s, "s%d" % h) for h in range(4)], [cstk(s, "r")])
                for h in range(4):
                    ts("dve", pb_[0:TP, h, :], pf_[0:TP, h, :], cs2[0:TP, 12 + h:13 + h], None, ALU.mult, None,
                       [pfk + "#%d" % h, cstk(s, "r")], [pbk + "#%d" % h])
                for h in range(4):
                    b = h
                    for mc in range(2):
                        tr(psb[b][:, mc * 128:mc * 128 + TP], pb_[0:TP, h, mc * 128:(mc + 1) * 128], identb[0:TP, 0:TP],
                           [pbk + "#%d" % h, "identb"], ["ps%d" % b])
                    src = psb[b][:, 0:256].rearrange("p (a t) -> p a t", a=2)[:, :, 0:TP]
                    cp("act" if h % 2 == 0 else "dve", prT[:, 2 * h:2 * h + 2, tsl], src, ["ps%d" % b], ["prT#%d_%d" % (h, s)])

            scores(0)
            for s in range(NS):
                if s + 1 < NS:
                    scores(s + 1)
                softmax_T(s)
            allpr = ["prT#%d_%d" % (h, s) for h in range(4) for s in range(NS)]
            for (csl2, mi) in cfg["att_full"]:
                mkfm_i, mvtm_i, mtok = cfg["mem"](mi)
                nr = csl2.stop - csl2.start
                for dch in range(8):
                    h = dch // 2
                    b = mmbank()
                    for mc in range(2):
                        mm(ps[b][:, 0:nr], mvtm_i[:, mc, dch * 128:(dch + 1) * 128], prT[:, 2 * h + mc, csl2],
                           [mtok + "v"] + allpr, ["ps%d" % b], start=(mc == 0), stop=(mc == 1))
                    cp("act" if dch % 2 == 0 else "dve", fm[1][:, dch, csl2], ps[b][:, 0:nr], ["ps%d" % b], ["fm1"])
            resid_proj(cfg, fm[1], "fm1", "w_co")

        def ffn(cfg):
            T, NS, TP = cfg["T"], cfg["NS"], cfg["TP"]
            xb, xtok = cfg["xb"], cfg["xtok"]
            bump[0] = 0
            actf = aalloc("actf", (22, T), BF16)
            h1s = [aalloc("h1s%d" % i, (1, T), F32) for i in range(2)]
            sgs = [aalloc("sgs%d" % i, (1, T), F32) for i in range(2)]
            rms_to_fm(xb, xtok, NS, TP, "g_ffn", fm[0], "fm0")
            cnt = {"n": 0}
            blocks = [(c0, min(512, DFF - c0)) for c0 in range(0, DFF, 512)]
            for (c0, ncols) in blocks:
                wt1, wk1 = wload("w_ff1", 0, 8, c0, ncols)
                wt3, wk3 = wload("w_ff3", 0, 8, c0, ncols)
                for m in range(ncols // 128):
                    fch = c0 // 128 + m
                    i = cnt["n"] % 2
                    cnt["n"] += 1
                    h1, sg_ = h1s[i][:, 0, :], sgs[i][:, 0, :]
                    hk, sk = "h1s%d" % i, "sgs%d" % i
                    b1 = mmbank()
                    for kc in range(8):
                        mm(ps[b1][:, 0:T], wt1[:, kc, m * 128:(m + 1) * 128], fm[0][:, kc, 0:T], ["fm0", wk1], ["ps%d" % b1],
                           start=(kc == 0), stop=(kc == 7))
                    b3 = mmbank()
                    for kc in range(8):
                        mm(ps[b3][:, 0:T], wt3[:, kc, m * 128:(m + 1) * 128], fm[0][:, kc, 0:T], ["fm0", wk3], ["ps%d" % b3],
                           start=(kc == 0), stop=(kc == 7))
                    act(sg_, ps[b1][:, 0:T], AF.Exp, ["ps%d" % b1], [sk], scale=-1.0)
                    sigmoid_from_exp(sg_, sk)
                    tt("dve", h1, ps[b1][:, 0:T], sg_, ALU.mult, ["ps%d" % b1, sk], [hk])
                    tt("dve", actf[:, fch, :], ps[b3][:, 0:T], h1, ALU.mult, ["ps%d" % b3, hk], ["actf#%d" % fch])
            agen = None
            if cfg.get("next") is not None:
                cfg["next"]["a_phase_back"] = "F2"
                agen = phase_a(cfg["next"])
                next(agen)
                cfg["next"]["a_done"] = True
            S.phase = "F2"
            for half in range(2):
                banks = [mmbank() for _ in range(NS)]
                kgroups = [(0, 8), (8, 8), (16, 6)]
                for gi, (k0, nk) in enumerate(kgroups):
                    wt, wk = wload("w_ff2", k0, nk, half * 512, 512)
                    for s in range(NS):
                        b = banks[s]
                        for kk_ in range(nk):
                            kc = k0 + kk_
                            mm(ps[b][0:TP, :], actf[:, kc, s * TP:(s + 1) * TP], wt[:, kk_, :], ["actf#%d" % kc, wk], ["ps%d" % b],
                               start=(kc == 0), stop=(kc == 21))
                if half == 0 and agen is not None:
                    for _ in agen:
                        pass
                    S.phase = "F2"
                for s in range(NS):
                    b = banks[s]
                    dst = xb[0:TP, s, half * 512:(half + 1) * 512]
                    tt("dve", dst, ps[b][0:TP, :], dst, ALU.add, ["ps%d" % b, xtok + "#%d" % s], [xtok + "#%d" % s])
            S.phase = "FIN"
            rms_stats(xb, xtok, NS, TP)
            o_nf = VB_COLS["nf"]
            for s in range(NS):
                stt(xb[0:TP, s, :], xb[0:TP, s, :], stat[0:TP, 4 + s:5 + s], vbb[0:TP, o_nf:o_nf + D], ALU.mult, ALU.mult,
                    [xtok + "#%d" % s, "stat#rs", "vbb"], [xtok + "#%d" % s])
                S.dma("act", cfg["ydst"][s * TP:(s + 1) * TP, :], xb[0:TP, s, :], reads=[xtok + "#%d" % s],
                      semkey=xtok + "_st", store=True)

        def phase_a(cfg):
            NS, TP = cfg["NS"], cfg["TP"]
            xb, xtok = cfg["xb"], cfg["xtok"]
            S.dma("sp", xb[0:TP, 0:NS, :], cfg["xsrc"].rearrange("(s p) d -> p s d", p=TP),
                  writes=[xtok + "#%d" % s for s in range(NS)], semkey=xtok + "_ld")
            S.phase = "A"
            for _ in rms_to_fm_gen(xb, xtok, NS, TP, "g_mix", fm[0], "fm0"):
                S.phase = cfg.get("a_phase_back", "A")
                yield
                S.phase = "A"

        def macro_tile(cfg):
            T, NS, TP, C = cfg["T"], cfg["NS"], cfg["TP"], cfg["C"]
            xb, xtok = cfg["xb"], cfg["xtok"]
            bump[0] = 0
            if not cfg.get("a_done"):
                for _ in phase_a(cfg):
                    pass
            S.phase = "B"

            rkv = aalloc("rkv", (12, T), BF16)
            lora = aalloc("lora", (2, T), F32)
            rwg = aalloc("rwg", (NS, 512), BF16)
            cfg["mark1"] = bump[0]
            hq = aalloc("hq", (4, T), BF16)
            hE = aalloc("hE", (4, T), F32)
            hi = aalloc("hi", (NS, 512), BF16)
            hsg = aalloc("hsg", (NS, 512), BF16)
            cfg["mark2"] = bump[0]
            pf = [aalloc("pf%d" % i, (1, T + 4), F32) for i in range(2)]
            dif = [aalloc("dif%d" % i, (1, T), F32) for i in range(2)]
            tg = [aalloc("tg0", (1, 512), F32)] * 2
            cnt = {"pf": 0, "tg": 0}

            def evac_rw(cbase):
                def f(m, pv, ptok):
                    cidx = cbase + m
                    i = cnt["pf"] % 2
                    cnt["pf"] += 1
                    pfb, dfb = pf[i][:, 0, :], dif[i][:, 0, :]
                    pk, dk = "pf%d" % i, "dif%d" % i
                    act(pfb[:, 1:T + 1], pv, AF.Copy, [ptok], [pk + "#m"])
                    if cfg["sample"]:
                        cp("pool", pfb[:, 0:1], shs[:, cidx:cidx + 1], ["shs"], [pk + "#c"])
                        tt("dve", dfb[:, 0:T], pfb[:, 0:T], pv, ALU.subtract, [pk + "#m", pk + "#c", ptok], [dk])
                        tt("dve", dfb[:, 32:33], shs[:, 14 + cidx:15 + cidx], pfb[:, 33:34], ALU.subtract,
                           [pk + "#m", "shs", dk], [dk])
                        cp("pool", sho[:, cidx:cidx + 1], pfb[:, 32:33], [pk + "#m"], ["sho#a%d" % cidx])
                        cp("pool", sho[:, 14 + cidx:15 + cidx], pfb[:, 64:65], [pk + "#m"], ["sho#b%d" % cidx])
                    else:
                        cp("pool", pfb[:, 0:1], carry[:, cidx:cidx + 1], ["carry#%d" % cidx], [pk + "#c"])
                        tt("dve", dfb[:, 0:T], pfb[:, 0:T], pv, ALU.subtract, [pk + "#m", pk + "#c", ptok], [dk])
                        cp("pool", carry[:, cidx:cidx + 1], pfb[:, T:T + 1], [pk + "#m", pk + "#c"], ["carry#%d" % cidx])
                    if cidx < 12:
                        dst, dtk = rkv[:, cidx, :], "rkv#%d" % cidx
                    else:
                        dst, dtk = lora[:, cidx - 12, :], "lora#%d" % (cidx - 12)
                    stt(dst, dfb[:, 0:T], vcol("mu", cidx), pv, ALU.mult, ALU.add, [dk, ptok] + VPR, [dtk])
                return f

            def evac_q(m, pv, ptok):
                cp("act", hq[:, m, :], pv, [ptok], ["hq#%d" % m])

            def evac_f(m, pv, ptok):
                act(hE[:, m, :], pv, AF.Exp, [ptok], ["hE#%d" % m], scale=-1.0)

            def evac_i(s, pv, ptok):
                cp("act", hi[0:TP, s, :], pv, [ptok], ["hi#%d" % s])

            def evac_g(s, pv, ptok):
                i = cnt["tg"] % 2
                cnt["tg"] += 1
                t1, tk = tg[i][0:TP, 0, :], "tg0"
                act(t1, pv, AF.Exp, [ptok], [tk], scale=-1.0)
                sigmoid_from_exp(t1, tk)
                tt("dve", t1, pv, t1, ALU.mult, [tk, ptok], [tk])
                o = VB_COLS["hgn"]
                tt("pool", hsg[0:TP, s, :], t1, vbb[0:TP, o:o + 512], ALU.mult, [tk, "vbb"], ["hsg#%d" % s])

            wt, wk = wload("w_in", 0, 8, 0, 512)
            proj_fm(fm[0], "fm0", wt, wk, 512, T, evac_q)
            wt, wk = wload("w_in", 0, 8, 512, 512)
            proj_fm(fm[0], "fm0", wt, wk, 512, T, evac_f)
            wt, wk = wload("w_in", 0, 8, 1024, 512)
            proj_tm(fm[0], "fm0", wt, wk, 512, NS, TP, evac_i)
            wt, wk = wload("w_in", 0, 8, 1536, 512)
            proj_tm(fm[0], "fm0", wt, wk, 512, NS, TP, evac_g)
            hgen = hgrn_mixer(cfg, hq, hE, hi, hsg)
            next(hgen)
            S.phase = "B"
            wt, wk = wload("w_in", 0, 8, 3584, 256)
            proj_fm(fm[0], "fm0", wt, wk, 256, T, evac_rw(12))
            next(hgen)
            for j in range(3):
                S.phase = "B"
                wt, wk = wload("w_in", 0, 8, 2048 + 512 * j, 512)
                proj_fm(fm[0], "fm0", wt, wk, 512, T, evac_rw(4 * j))
                next(hgen)
            dbg_store("rkv", rkv[:, :, :], ["rkv"])
            dbg_store("lora", lora[:, :, :], ["lora"])
            if stop_after == "B":
                return
            S.phase = "HG"
            next(hgen)
            dbg_store("mixhg", mixtm[0:TP, 0:NS, 0:512], ["mixtm"])
            dbg_store("S_hg", cfg["S_hg"](0)[0][:, :, :], [cfg["S_hg"](0)[2] + "f"])
            if stop_after == "HG":
                return
            S.phase = "RWprep"
            rwkv_mixer(cfg, rkv, lora, rwg)
            dbg_store("mixrw", mixtm[0:TP, 0:NS, 512:1024], ["mixtm"])
            dbg_store("H_rw", cfg["H_rw"](0)[0][:, :, :], [cfg["H_rw"](0)[2] + "f"])
            if stop_after == "RW":
                return
            S.phase = "OUT"
            tm_to_fm(lambda s: mixtm[0:TP, s, :], lambda s: ["mixtm#h%d" % s, "mixtm#r%d" % s], NS, TP, fm[1], "fm1")
            resid_proj(cfg, fm[1], "fm1", "w_out")
            dbg_store("x1", xb[0:TP, 0:NS, :], [xtok])
            if stop_after == "OUT":
                return
            S.phase = "X"
            cross_attn(cfg)
            dbg_store("x2", xb[0:TP, 0:NS, :], [xtok])
            if stop_after == "X":
                return
            S.phase = "F1"
            ffn(cfg)

        def memory_kv():
            bump[0] = 0
            mtm = aalloc("mtm", (2, D), F32)
            S.dma("sp", mtm[:, :, :], I["mem"].rearrange("(s p) d -> p s d", p=128), writes=["mtm#0", "mtm#1"], semkey="mtm_ld")
            rms_to_fm(mtm, "mtm", 2, 128, "g_mem", fm[0], "fm0")
            okv = [aalloc("okv%d" % i, (1, 512), F32) for i in range(2)]
            cnt = {"n": 0}
            _mode = int(_os.environ.get('MEMKV_MODE', '9'))
            if _mode < 2:
                return
            for (wname, oname, isk) in (("w_ck", "o_mk", True), ("w_cv", "o_mv", False)):
                if _mode < 4 and not isk:
                    continue
                for half in range(2):
                    wt, wk = wload(wname, 0, 8, half * 512, 512)

                    def ev(s, pv, ptok, half=half, oname=oname, isk=isk):
                        i = cnt["n"] % 2
                        cnt["n"] += 1
                        ob, ok = okv[i][:, 0, :], "okv%d" % i
                        cp("act", ob, pv, [ptok], [ok])
                        S.dma("act", O[oname][s * 128:(s + 1) * 128, half * 512:(half + 1) * 512], ob, reads=[ok],
                              semkey=ok + "_st", store=True)
                        if not isk:
                            cp("dve", mvtm[:, s, half * 512:(half + 1) * 512], pv, [ptok], ["mem0v"])
                    proj_tm(fm[0], "fm0", wt, wk, 512, 2, 128, ev)
                    if isk and _mode >= 3:
                        def evk(m, pv, ptok, half=half):
                            cp("act", mkfm[:, half * 4 + m, :], pv, [ptok], ["mem0k"])
                        proj_fm(fm[0], "fm0", wt, wk, 512, NMEM, evk)

        if not _os.environ.get('SKIP_MEMKV'):
            memory_kv()
        S.i("pool", "memset", [], ["carry"], ap=carry[:], constant=0.0)
        S.i("pool", "memset", [], ["S_hg0f"], ap=S_hg[0][:], constant=0.0)
        S.i("pool", "memset", [], ["S_hg0b"], ap=S_hgb[0][:], constant=0.0)
        S.i("pool", "memset", [], ["H_rw0f"], ap=H_rw[0][:], constant=0.0)
        S.i("pool", "memset", [], ["H_rw0b"], ap=H_rwb[0][:], constant=0.0)
        cfgs = []
        for mt in range(NMT):
            cfg = dict(T=512, NS=4, TP=128, C=64, xb=xres[mt % 2], xtok="xres%d" % (mt % 2),
                       xsrc=I["xp"][mt * 512:(mt + 1) * 512, :], ydst=O["yp"][mt * 512:(mt + 1) * 512, :], sample=False)
            cfg["S_hg"] = lambda ch: (S_hg[0], S_hgb[0], "S_hg0")
            cfg["H_rw"] = lambda ch: (H_rw[0], H_rwb[0], "H_rw0")
            cfg["hf_chunks"] = (7,) if mt == NMT - 1 else ()
            cfg["att_groups"] = lambda s: [(slice(0, 128), 0)]
            cfg["mem"] = lambda mi: (mkfm, mvtm, "mem0")
            cfg["att_full"] = [(slice(0, 512), 0)]
            cfgs.append(cfg)
        for mt in range(NMT - 1):
            cfgs[mt]["next"] = cfgs[mt + 1]
        for mt in range(NMT):
            macro_tile(cfgs[mt])
        if _os.environ.get('SKIP_TAIL'):
            info = S.emit()
            return nc, info
        S.dma("act", O["o_p_hg"].rearrange("h k v -> k h v"), S_hg[0][:, :, :], reads=["S_hg0f"], semkey="o_p_hg", store=True)
        S.dma("act", O["o_p_sh"][:, :], carry[:, :], reads=["carry"], semkey="o_p_sh", store=True)

        def rw_state_out(Hf, htok, dst, tag):
            for hp in range(4):
                b = mmbank()
                S.i("pe", "transpose", [htok + "f#%d" % hp, "identf"], ["ps%d" % b], out=ps[b][:, 0:128], in_=Hf[:, hp, :],
                    identity=identf[:, :])
                so = stt_out[tag][:, hp, :]
                cp("act", so, ps[b][:, 0:128], ["ps%d" % b], ["so_%s#%d" % (tag, hp)])
                for h in range(2):
                    S.dma("act", dst[2 * hp + h, :, :], so[h * 64:(h + 1) * 64, h * 64:(h + 1) * 64],
                          reads=["so_%s#%d" % (tag, hp)], semkey="so_%s" % tag, store=True)
        bump[0] = 0
        stt_out = {"p": aalloc("so_p", (4, 128), F32)}
        rw_state_out(H_rw[0], "H_rw0", O["o_p_rw"], "p")
        if sample:
            S.dma("sp", shs[:, :], I["sh_sh"][:, :], writes=["shs"], semkey="shs_ld")
            spad = aalloc("spad", (4, 128), F32)
            for q in range(2):
                S.dma("sp", S_hg[1 + q][:, :, :], I["sh_hg"][q].rearrange("h k v -> k h v"), writes=["S_hg%df" % (1 + q)],
                      semkey="shg_ld%d" % q)
                cp("dve", S_hgb[1 + q][:, :, :], S_hg[1 + q][:, :, :], ["S_hg%df" % (1 + q)], ["S_hg%db" % (1 + q)])
                S.i("pool", "memset", [], ["spad"], ap=spad[:, :, :], constant=0.0)
                for hp in range(4):
                    for h in range(2):
                        S.dma("sp", spad[h * 64:(h + 1) * 64, hp, h * 64:(h + 1) * 64], I["sh_rw"][q, 2 * hp + h, :, :],
                              reads=[], writes=["spad#%d" % hp], semkey="srw_ld%d" % hp)
                for hp in range(4):
                    b = mmbank()
                    S.i("pe", "transpose", ["spad#%d" % hp, "identf"], ["ps%d" % b], out=ps[b][:, 0:128], in_=spad[:, hp, :],
                        identity=identf[:, :])
                    cp("act", H_rw[1 + q][:, hp, :], ps[b][:, 0:128], ["ps%d" % b], ["H_rw%df#%d" % (1 + q, hp)])
                    cp("dve", H_rwb[1 + q][:, hp, :], ps[b][:, 0:128], ["ps%d" % b], ["H_rw%db#%d" % (1 + q, hp)])
            xi = NMT % 2
            cfg = dict(T=64, NS=1, TP=64, C=32, xb=xres[xi], xtok="xres%d" % xi, xsrc=I["xs"], ydst=O["ys"], sample=True)
            cfg["S_hg"] = lambda ch: (S_hg[1 + ch], S_hgb[1 + ch], "S_hg%d" % (1 + ch))
            cfg["H_rw"] = lambda ch: (H_rw[1 + ch], H_rwb[1 + ch], "H_rw%d" % (1 + ch))
            cfg["hf_chunks"] = None
            cfg["att_groups"] = lambda s: [(slice(0, 32), 0), (slice(32, 64), 1)]
            cfg["att_full"] = [(slice(0, 32), 0), (slice(32, 64), 1)]
            macro_tile(cfg)
            bump[0] = 0
            stt_out["s0"] = aalloc("so_s0", (4, 128), F32)
            stt_out["s1"] = aalloc("so_s1", (4, 128), F32)
            for q in range(2):
                S.dma("act", O["o_s_hg"][q].rearrange("h k v -> k h v"), S_hg[1 + q][:, :, :], reads=["S_hg%df" % (1 + q)],
                      semkey="o_s_hg%d" % q, store=True)
                rw_state_out(H_rw[1 + q], "H_rw%d" % (1 + q), O["o_s_rw"][q], "s%d" % q)
            S.dma("act", O["o_s_sh"][:, :], sho[:, :], reads=["sho"], semkey="o_s_sh", store=True)
        info = S.emit()
    return nc, info


def host_inputs(inp, core, NMT=8):
    f = lambda a: np.ascontiguousarray(np.asarray(a, np.float32))
    m = {}
    m["xp"] = f(inp["x_prompt"][core][:512 * NMT])
    m["xs"] = f(inp["x_sample"][2 * core:2 * core + 2].reshape(64, D))
    m["mem"] = f(inp["mem_prompt"][core])
    m["ck"] = f(inp["cache_mem_k"][0, 2 * core:2 * core + 2].reshape(2, NMEM, D))
    m["cv"] = f(inp["cache_mem_v"][0, 2 * core:2 * core + 2].reshape(2, NMEM, D))
    m["sh_hg"] = f(inp["state_hgrn"][0, 2 * core:2 * core + 2])
    m["sh_rw"] = f(inp["state_rwkv"][0, 2 * core:2 * core + 2])
    sh = np.asarray(inp["state_rwkv_shift"], np.float32)[0, 2 * core:2 * core + 2, 0]
    m["sh_sh"] = f(np.concatenate([_chunks(sh[0]), _chunks(sh[1])], 1))
    for k in ("w_in", "w_out", "w_cq", "w_ck", "w_cv", "w_co", "w_ff1", "w_ff3", "w_ff2"):
        m[k] = f(inp[k][0])
    m["w_b"] = f(inp["rw_w_b"][0])
    m["a_b"] = f(inp["rw_a_b"][0])
    m["g_b"] = f(inp["rw_g_b"][0])
    vp = np.zeros((128, NVP_IN), np.float32)

    def put(name, v):
        c = _chunks(v)
        vp[:, VP_COLS[name]:VP_COLS[name] + c.shape[1]] = c
    put("g_mix", inp["norm_mix"][0]); put("g_cross", inp["norm_cross"][0]); put("g_ffn", inp["norm_ffn"][0])
    put("g_mem", inp["norm_mem"][0])
    put("lb0", inp["hgrn_lb_logits"][0]); put("lb1", inp["hgrn_lb_logits"][1])
    put("mu", inp["rw_mu"][0]); put("w0", inp["rw_w0"][0]); put("a0", inp["rw_a0"][0])
    put("kk", inp["rw_k_k"][0]); put("ka", inp["rw_k_a"][0]); put("rk", np.asarray(inp["rw_r_k"][0]).reshape(-1))
    m["vp"] = vp
    vb = np.zeros((1, NVB), np.float32)
    vb[0, 0:512] = inp["hgrn_norm"][0]; vb[0, 512:1024] = inp["rw_gn_w"][0]; vb[0, 1024:1536] = inp["rw_gn_b"][0]
    vb[0, 1536:2560] = inp["norm_final"]
    m["vb"] = vb
    m["ident"] = np.eye(128, dtype=np.float32)
    p = np.arange(128)
    m["e64"] = (p[:, None] // 64 == p[None, :] // 64).astype(np.float32)
    m["ehead"] = (p[:, None] // 64 == np.arange(2)[None, :]).astype(np.float32)
    r64 = np.ones((1, 512), np.float32); r64[0, ::64] = 0
    r32 = np.ones((1, 64), np.float32); r32[0, ::32] = 0
    m["rmask64"] = r64
    m["rmask32"] = r32
    for k, v in make_masks().items():
        m["m_" + k] = v
    return m


_CACHE = {}


def kernel(**inputs):
    inp = {k: np.asarray(v) for k, v in inputs.items()}
    if "nc" not in _CACHE:
        _CACHE["nc"] = build_program(NMT=8, sample=True)[0]
    nc = _CACHE["nc"]
    maps = [host_inputs(inp, c, NMT=8) for c in range(8)]
    res = run_bass_kernel_spmd(nc, maps, core_ids=list(range(8)))
    R = res.results
    f = lambda a: np.ascontiguousarray(np.asarray(a, np.float32))
    unch = lambda a: f(a).T.reshape(-1)
    y_prompt = np.stack([f(R[c]["yp"]) for c in range(8)], 0)
    y_sample = np.concatenate([f(R[c]["ys"]).reshape(2, 32, D) for c in range(8)], 0)
    p_hg = np.stack([f(R[c]["o_p_hg"]) for c in range(8)], 0)[None]
    p_rw = np.stack([f(R[c]["o_p_rw"]) for c in range(8)], 0)[None]
    p_sh = np.stack([unch(R[c]["o_p_sh"]) for c in range(8)], 0).reshape(1, 8, 1, RWC)
    p_mk = np.stack([f(R[c]["o_mk"]).reshape(NMEM, 4, 256) for c in range(8)], 0)[None]
    p_mv = np.stack([f(R[c]["o_mv"]).reshape(NMEM, 4, 256) for c in range(8)], 0)[None]
    s_hg = np.concatenate([f(R[c]["o_s_hg"]) for c in range(8)], 0)[None]
    s_rw = np.concatenate([f(R[c]["o_s_rw"]) for c in range(8)], 0)[None]
    s_sh = np.stack([unch(f(R[c]["o_s_sh"])[:, 14 * q:14 * q + 14]) for c in range(8) for q in range(2)], 0)
    s_sh = s_sh.reshape(1, 16, 1, RWC)
    return (y_prompt, y_sample, p_hg, p_rw, p_sh, p_mk, p_mv, s_hg, s_rw, s_sh)
```

```python
from contextlib import ExitStack
import os as _os
import numpy as np
import concourse.bass as bass
import concourse.mybir as mybir
from concourse.bass_utils import run_bass_kernel_spmd

F32 = mybir.dt.float32
BF16 = mybir.dt.bfloat16
AF = mybir.ActivationFunctionType
ALU = mybir.AluOpType

D = 1024
DFF = 2816
NMEM = 256
RWC = 1792
RMS_EPS = 1e-6
GN_EPS = 64e-5
DECAY_K = 0.6065306597126334


class _Op:
    __slots__ = ("eng", "fn", "deps", "dma", "semkey", "sem", "val", "signal", "ph")

    def __init__(self, eng, fn, dma, semkey):
        self.eng = eng
        self.fn = fn
        self.deps = []
        self.dma = dma
        self.semkey = semkey
        self.sem = None
        self.val = 0
        self.signal = False


class Sched:
    ENGS = ("pe", "act", "dve", "pool", "sp")

    def __init__(self, nc, es):
        self.nc = nc
        self.es = es
        self.ops = []
        self.res = {}
        self.bases = {}
        self.ranges = {}
        self.dma_cnt = {}
        self.stores = []
        self.phase = ""

    def set_range(self, base, lo, hi):
        self.ranges[base] = (lo, hi)

    def _conf(self, tok):
        base = tok.split("#")[0]
        out = []
        whole = ("#" not in tok) or base.startswith("ps")
        for t in self.bases.get(base, ()):
            if t == tok or whole or "#" not in t:
                out.append(t)
        rg = self.ranges.get(base)
        if rg is not None:
            for b2, r2 in self.ranges.items():
                if b2 != base and r2[0] < rg[1] and rg[0] < r2[1]:
                    out.extend(self.bases.get(b2, ()))
        return out

    def _dep(self, op, prod, raw):
        if prod is op or prod is None:
            return
        if not op.dma and not prod.dma and op.eng == prod.eng:
            if op.eng == "pe":
                return
        if prod not in op.deps:
            op.deps.append(prod)

    def op(self, eng, fn, reads=(), writes=(), dma=False, semkey=None, store=False):
        o = _Op(eng, fn, dma, semkey)
        o.ph = self.phase
        psr = [r for r in reads if r.startswith("ps")]
        if psr:
            reads = [r for r in reads if not r.startswith("ps")]
            writes = list(writes) + [r for r in psr if r not in writes]
            o_psr = True
        else:
            o_psr = False
        for r in reads:
            for t in self._conf(r):
                st = self.res.get(t)
                if st is not None:
                    self._dep(o, st[0], True)
        for w in writes:
            isps = w.startswith("ps")
            for t in self._conf(w):
                st = self.res.get(t)
                if st is not None:
                    self._dep(o, st[0], isps and o_psr)
                    for rd in st[1]:
                        self._dep(o, rd, False)
        for r in reads:
            st = self.res.get(r)
            if st is None:
                st = self.res[r] = [None, []]
                self.bases.setdefault(r.split("#")[0], set()).add(r)
            if not dma:
                st[1] = [x for x in st[1] if x.dma or x.eng != eng]
            st[1].append(o)
        for w in writes:
            self.bases.setdefault(w.split("#")[0], set()).add(w)
            self.res[w] = [o, []]
        if dma:
            o.signal = True
        self.ops.append(o)
        if store:
            self.stores.append(o)
        return o

    def i(self, eng, name, reads, writes, **kw):
        def fn(e, name=name, kw=kw):
            return getattr(e, name)(**kw)
        return self.op(eng, fn, reads, writes)

    def dma(self, q, out, in_, reads=(), writes=(), semkey=None, store=False, **kw):
        return self.op(q, lambda e: e.dma_start(out=out, in_=in_, **kw), reads, writes,
                       dma=True, semkey=semkey, store=store)

    def emit(self):
        nc, es = self.nc, self.es
        for o in self.ops:
            for p in o.deps:
                p.signal = True
        esem = {e: es.enter_context(nc.semaphore("s_" + e)) for e in self.ENGS}
        dsem = {}
        cnt = {e: 0 for e in self.ENGS}
        for o in self.ops:
            if o.dma:
                if o.semkey not in dsem:
                    dsem[o.semkey] = es.enter_context(nc.semaphore("d_%d" % len(dsem)))
                    self.dma_cnt[o.semkey] = 0
                self.dma_cnt[o.semkey] += 16
                o.sem = dsem[o.semkey]
                o.val = self.dma_cnt[o.semkey]
            elif o.signal:
                cnt[o.eng] += 1
                o.sem = esem[o.eng]
                o.val = cnt[o.eng]
        per = {e: [o for o in self.ops if o.eng == e] for e in self.ENGS}
        if _os.environ.get("MK_PHASES"):
            import json
            json.dump({e: [o.ph for o in per[e] if not o.dma] for e in self.ENGS}, open(_os.environ["MK_PHASES"], "w"))
        finals = {}
        for o in self.ops:
            if o.dma:
                finals[id(o.sem)] = (o.sem, max(finals.get(id(o.sem), (None, 0))[1], o.val))

        def run(e, h, last=False):
            seen = {}
            for o in per[e]:
                for p in o.deps:
                    k = id(p.sem)
                    if seen.get(k, 0) < p.val:
                        h.wait_ge(p.sem, p.val)
                        seen[k] = p.val
                inst = o.fn(h)
                if o.signal:
                    inst.then_inc(o.sem, 16 if o.dma else 1)
            if last:
                for sem, v in finals.values():
                    h.wait_ge(sem, v)

        with nc.Block() as block:
            @block.tensor
            def _(h):
                run("pe", h)

            @block.scalar
            def _(h):
                run("act", h)

            @block.vector
            def _(h):
                run("dve", h)

            @block.gpsimd
            def _(h):
                run("pool", h)

            @block.sync
            def _(h):
                run("sp", h, last=True)
        return {e: len(per[e]) for e in self.ENGS}, len(dsem)


VP_COLS = {}
_off = 0
for _n, _c in (("g_mix", 8), ("g_cross", 8), ("g_ffn", 8), ("g_mem", 8), ("lb0", 4), ("lb1", 4), ("mu", 14),
               ("w0", 4), ("a0", 4), ("kk", 4), ("ka", 4), ("rk", 4)):
    VP_COLS[_n] = _off
    _off += _c
NVP_IN = _off
for _n, _c in (("lb", 4), ("oml", 4), ("noml", 4), ("nw0", 4), ("na0", 4), ("omka", 4)):
    VP_COLS[_n] = _off
    _off += _c
NVP = _off
VB_COLS = {"hgn": 0, "gnw": 512, "gnb": 1024, "nf": 1536}
NVB = 2560


def _chunks(v):
    v = np.asarray(v, np.float32).reshape(-1, 128)
    return np.ascontiguousarray(v.T)


def make_masks():
    m = {}
    p = np.arange(128)[:, None]
    q = np.arange(128)[None, :]
    for C in (64, 32):
        HC = 2 * C
        m["hg%d" % C] = (((p // C) == (q // C)) & (p <= q)).astype(np.float32)
        inb = (p < HC) & (q < HC) & ((p // C) == (q // C))
        t, u = p % C, q % C
        m["a1_%d" % C] = -(inb & (t > u)).astype(np.float32)
        j = np.arange(C)[None, :]
        su = (inb & (t < u)).astype(np.float32)[:, 0:HC]
        iu = ((t <= j) & (p < HC)).astype(np.float32) * np.ones((128, C), np.float32)
        m["a2_%d" % C] = np.concatenate([-su, iu], 1)
        m["a3_%d" % C] = np.concatenate([su, iu], 1)
    return m


def build_program(NMT=8, sample=True, dbg=(), stop_after=None):
    nc = bass.Bass("TRN2", target_bir_lowering=False)
    dt_in = lambda n, s: nc.dram_tensor(n, list(s), F32, kind="ExternalInput").ap()
    dt_out = lambda n, s: nc.dram_tensor(n, list(s), F32, kind="ExternalOutput").ap()
    TPROMPT = 512 * NMT
    I = dict(
        xp=dt_in("xp", (TPROMPT, D)), xs=dt_in("xs", (64, D)), mem=dt_in("mem", (NMEM, D)),
        ck=dt_in("ck", (2, NMEM, D)), cv=dt_in("cv", (2, NMEM, D)),
        sh_hg=dt_in("sh_hg", (2, 4, 128, 128)), sh_rw=dt_in("sh_rw", (2, 8, 64, 64)), sh_sh=dt_in("sh_sh", (128, 28)),
        w_in=dt_in("w_in", (D, 3840)), w_out=dt_in("w_out", (D, D)), w_cq=dt_in("w_cq", (D, D)),
        w_ck=dt_in("w_ck", (D, D)), w_cv=dt_in("w_cv", (D, D)), w_co=dt_in("w_co", (D, D)),
        w_ff1=dt_in("w_ff1", (D, DFF)), w_ff3=dt_in("w_ff3", (D, DFF)), w_ff2=dt_in("w_ff2", (DFF, D)),
        w_b=dt_in("w_b", (64, 512)), a_b=dt_in("a_b", (64, 512)), g_b=dt_in("g_b", (128, 512)),
        vp=dt_in("vp", (128, NVP_IN)), vb=dt_in("vb", (1, NVB)),
        ident=dt_in("ident", (128, 128)), e64=dt_in("e64", (128, 128)), ehead=dt_in("ehead", (128, 2)),
        rmask64=dt_in("rmask64", (1, 512)), rmask32=dt_in("rmask32", (1, 64)),
    )
    masks = make_masks()
    for k, v in masks.items():
        I["m_" + k] = dt_in("m_" + k, v.shape)
    O = dict(
        yp=dt_out("yp", (TPROMPT, D)), ys=dt_out("ys", (64, D)),
        o_p_hg=dt_out("o_p_hg", (4, 128, 128)), o_p_rw=dt_out("o_p_rw", (8, 64, 64)), o_p_sh=dt_out("o_p_sh", (128, 14)),
        o_mk=dt_out("o_mk", (NMEM, D)), o_mv=dt_out("o_mv", (NMEM, D)),
        o_s_hg=dt_out("o_s_hg", (2, 4, 128, 128)), o_s_rw=dt_out("o_s_rw", (2, 8, 64, 64)),
        o_s_sh=dt_out("o_s_sh", (128, 28)),
    )
    DBG = {}
    for name, shape in dbg:
        DBG[name] = dt_out("dbg_" + name, shape)

    es = ExitStack()
    with es:
        S = Sched(nc, es)
        sbt = lambda n, s, d: es.enter_context(nc.sbuf_tensor("sb_" + n, list(s), d))

        xres = [sbt("xres%d" % i, (128, 4, D), F32) for i in range(2)]
        NWS = 3
        wsl = [sbt("wsl%d" % i, (128, 8, 512), BF16) for i in range(NWS)]
        fm = [sbt("fm%d" % i, (128, 8, 512), BF16) for i in range(2)]
        ntm = [sbt("ntm%d" % i, (128, D), BF16) for i in range(2)]
        mixtm = sbt("mixtm", (128, 4, D), BF16)
        vp = sbt("vp", (128, NVP), F32)
        vbb = sbt("vbb", (128, NVB), F32)
        identb = sbt("identb", (128, 128), BF16)
        identf = sbt("identf", (128, 128), F32)
        e64 = sbt("e64", (128, 128), BF16)
        ehead = sbt("ehead", (128, 2), BF16)
        wlo = sbt("wlo", (128, 2, 512), BF16)
        rmask = {64: sbt("rmask64", (128, 512), F32), 32: sbt("rmask32", (128, 64), F32)}
        mk = {}
        for k, v in masks.items():
            mk[k] = sbt("m_" + k, (128, v.shape[1]), BF16)
        stat = sbt("stat", (128, 64), F32)
        carry = sbt("carry", (128, 14), F32)
        shs = sbt("shs", (128, 28), F32)
        sho = sbt("sho", (128, 28), F32)
        S_hg = [sbt("S_hg%d" % i, (128, 4, 128), F32) for i in range(3)]
        S_hgb = [sbt("S_hgb%d" % i, (128, 4, 128), BF16) for i in range(3)]
        H_rw = [sbt("H_rw%d" % i, (128, 4, 128), F32) for i in range(3)]
        H_rwb = [sbt("H_rwb%d" % i, (128, 4, 128), BF16) for i in range(3)]
        mkfm = sbt("mkfm", (128, 8, NMEM), BF16)
        mvtm = sbt("mvtm", (128, 2, D), BF16)

        SA = nc.sbuf_bytes_remaining - 64
        SA -= SA % 64
        arena = sbt("arena", (128, SA // 2), BF16)
        bump = [0]

        def aalloc(name, shape, dtype):
            n = int(np.prod(shape))
            nb = n * (4 if dtype == F32 else 2)
            lo = bump[0]
            bump[0] = lo + ((nb + 31) // 32) * 32
            assert bump[0] <= SA, (name, bump[0], SA)
            S.set_range(name, lo, lo + nb)
            v = arena[:, lo // 2: lo // 2 + nb // 2]
            if dtype == F32:
                v = v.bitcast(F32)
            if len(shape) == 2:
                v = v.rearrange("p (a b) -> p a b", a=shape[0])
            elif len(shape) == 3:
                v = v.rearrange("p (a b c) -> p a b c", a=shape[0], b=shape[1])
            return v

        ps = [es.enter_context(nc.psum_tensor("ps%d" % i, [128, 512], F32)) for i in range(8)]
        psb = [p[:].bitcast(BF16) for p in ps]

        def vcol(name, i=0):
            c = VP_COLS[name] + i
            return vp[:, c:c + 1]

        def act(out, in_, func, r, w, **kw):
            S.i("act", "activation", r, w, out=out, in_=in_, func=func, **kw)

        def mm(out, lhsT, rhs, r, w, start=True, stop=True, **kw):
            S.i("pe", "matmul", r, w, out=out, lhsT=lhsT, rhs=rhs, start=start, stop=stop, **kw)

        def tr(out, in_, ident, r, w):
            S.i("pe", "transpose", r, w, out=out, in_=in_, identity=ident)

        def tt(eng, out, in0, in1, op, r, w):
            S.i(eng, "tensor_tensor", r, w, out=out, in0=in0, in1=in1, op=op)

        def ts(eng, out, in0, s1, s2, op0, op1, r, w):
            if s2 is None:
                S.i(eng, "tensor_scalar", r, w, out=out, in0=in0, scalar1=s1, scalar2=None, op0=op0)
            else:
                S.i(eng, "tensor_scalar", r, w, out=out, in0=in0, scalar1=s1, scalar2=s2, op0=op0, op1=op1)

        def stt(out, in0, scalar, in1, op0, op1, r, w):
            S.i("dve", "scalar_tensor_tensor", r, w, out=out, in0=in0, scalar=scalar, in1=in1, op0=op0, op1=op1)

        def cp(eng, out, in_, r, w):
            if eng == "act":
                act(out, in_, AF.Copy, r, w)
            else:
                S.i(eng, "tensor_copy", r, w, out=out, in_=in_)

        def recip(out, in_, r, w):
            S.i("dve", "reciprocal", r, w, out=out, in_=in_)

        def sigmoid_from_exp(buf, tok, out=None, otok=None):
            act(buf, buf, AF.Ln, [tok], [tok], bias=1.0)
            if out is None:
                act(buf, buf, AF.Exp, [tok], [tok], scale=-1.0)
            else:
                act(out, buf, AF.Exp, [tok], [otok], scale=-1.0)

        def rsqrt_chain(dst, src, rtoks, wtok, scale, eps):
            act(dst, src, AF.Ln, rtoks, [wtok], scale=scale, bias=eps)
            act(dst, dst, AF.Exp, [wtok], [wtok], scale=-0.5)

        def dbg_store(name, src_ap, reads):
            if name in DBG:
                S.dma("pool", DBG[name], src_ap, reads=reads, semkey="dbg_" + name, store=True)

        for b_ in range(8):
            S.i("dve", "memset", [], ["ps%d" % b_], ap=ps[b_][:], constant=0.0)
        S.dma("sp", vp[:, 0:NVP_IN], I["vp"][:, :], writes=["vp"], semkey="c_vp")
        S.dma("sp", vbb[:], I["vb"].partition_broadcast(128), writes=["vbb"], semkey="c_vbb")
        S.dma("sp", identf[:], I["ident"][:, :], writes=["identf"], semkey="c_idf")
        S.dma("sp", rmask[64][:], I["rmask64"].partition_broadcast(128), writes=["rmask64"], semkey="c_rm64")
        S.dma("sp", rmask[32][:], I["rmask32"].partition_broadcast(128), writes=["rmask32"], semkey="c_rm32")
        S.dma("pool", identb[:], I["ident"][:, :], writes=["identb"], semkey="c_idb")
        S.dma("pool", e64[:], I["e64"][:, :], writes=["e64"], semkey="c_e64")
        S.dma("pool", ehead[:], I["ehead"][:, :], writes=["ehead"], semkey="c_eh")
        S.dma("pool", wlo[0:64, 0, :], I["w_b"][:, :], writes=["wlo#0"], semkey="c_wlo0")
        S.dma("pool", wlo[64:128, 0, :], I["a_b"][:, :], writes=["wlo#1"], semkey="c_wlo1")
        S.dma("pool", wlo[:, 1, :], I["g_b"][:, :], writes=["wlo#2"], semkey="c_wlo2")
        for k in masks:
            S.dma("pool", mk[k][:], I["m_" + k][:, :], writes=["m_" + k], semkey="c_m_" + k)

        def vps(name):
            c = VP_COLS[name]
            return vp[:, c:c + 4]
        tt("dve", vps("lb"), vps("lb1"), vps("lb0"), ALU.subtract, ["vp"], ["vp#lb"])
        act(vps("lb"), vps("lb"), AF.Exp, ["vp#lb"], ["vp#lb"])
        ts("dve", vps("lb"), vps("lb"), 1.0, None, ALU.add, None, ["vp#lb"], ["vp#lb"])
        recip(vps("lb"), vps("lb"), ["vp#lb"], ["vp#lb"])
        ts("dve", vps("oml"), vps("lb"), -1.0, 1.0, ALU.mult, ALU.add, ["vp#lb"], ["vp#oml"])
        ts("dve", vps("noml"), vps("lb"), 1.0, -1.0, ALU.mult, ALU.add, ["vp#lb"], ["vp#noml"])
        ts("dve", vps("nw0"), vps("w0"), -1.0, None, ALU.mult, None, ["vp"], ["vp#nw0"])
        ts("dve", vps("na0"), vps("a0"), -1.0, None, ALU.mult, None, ["vp"], ["vp#na0"])
        ts("dve", vps("omka"), vps("ka"), -1.0, 1.0, ALU.mult, ALU.add, ["vp"], ["vp#omka"])
        VPR = ["vp"]

        wstate = {"n": 0}
        wscr = {}

        def wconvert(wname, k0, nk, c0, ncols):
            wap = I[wname]
            if wname not in wscr:
                wscr[wname] = (nc.dram_tensor("scr_" + wname, list(wap.shape), BF16, kind="Internal").ap(), set())
            scr, done = wscr[wname]
            key = (k0, nk, c0, ncols)
            ctok = "scr_%s_%d_%d" % (wname, k0, c0)
            if key not in done:
                done.add(key)
                S.dma("pool", scr[k0 * 128:(k0 + nk) * 128, c0:c0 + ncols], wap[k0 * 128:(k0 + nk) * 128, c0:c0 + ncols],
                      writes=[ctok], semkey=ctok)

        def wconvert_all():
            for wn in ("w_ck", "w_cv"):
                for half in range(2):
                    wconvert(wn, 0, 8, half * 512, 512)
            for c0, nc_ in ((0, 512), (512, 512), (1024, 512), (1536, 512), (3584, 256), (2048, 512), (2560, 512), (3072, 512)):
                wconvert("w_in", 0, 8, c0, nc_)
            for wn in ("w_out", "w_cq", "w_co"):
                for half in range(2):
                    wconvert(wn, 0, 8, half * 512, 512)
            for c0 in range(0, DFF, 512):
                wconvert("w_ff1", 0, 8, c0, min(512, DFF - c0))
                wconvert("w_ff3", 0, 8, c0, min(512, DFF - c0))
            for half in range(2):
                for (k0, nk) in ((0, 8), (8, 8), (16, 6)):
                    wconvert("w_ff2", k0, nk, half * 512, 512)

        def wload(wname, k0, nk, c0, ncols):
            wap = I[wname]
            if wname not in wscr:
                wscr[wname] = (nc.dram_tensor("scr_" + wname, list(wap.shape), BF16, kind="Internal").ap(), set())
            scr, done = wscr[wname]
            key = (k0, nk, c0, ncols)
            ctok = "scr_%s_%d_%d" % (wname, k0, c0)
            if key not in done:
                done.add(key)
                S.dma("pool", scr[k0 * 128:(k0 + nk) * 128, c0:c0 + ncols], wap[k0 * 128:(k0 + nk) * 128, c0:c0 + ncols],
                      writes=[ctok], semkey=ctok)
            i = wstate["n"] % NWS
            wstate["n"] += 1
            src = scr[k0 * 128:(k0 + nk) * 128, c0:c0 + ncols].rearrange("(kc p) c -> p kc c", p=128)
            S.dma("sp", wsl[i][:, 0:nk, 0:ncols], src, reads=[ctok], writes=["wsl%d" % i], semkey="wsl%d" % i)
            return wsl[i], "wsl%d" % i

        mmrot = {"n": 0}

        def mmbank():
            b = 4 + (mmrot["n"] % 4)
            mmrot["n"] += 1
            return b

        def rms_stats(xb, xtok, NS, TP):
            for s in range(NS):
                act(mixtm[0:TP, s, :], xb[0:TP, s, :], AF.Square, [xtok + "#%d" % s], ["mixtm#h%d" % s, "mixtm#r%d" % s, "stat#ss%d" % s],
                    accum_out=stat[0:TP, s:s + 1])
            rsqrt_chain(stat[0:TP, 4:4 + NS], stat[0:TP, 0:NS], ["stat#ss%d" % s for s in range(NS)], "stat#rs",
                        1.0 / D, RMS_EPS)

        def rms_to_fm(xb, xtok, NS, TP, gname, dst, dtok):
            for _ in rms_to_fm_gen(xb, xtok, NS, TP, gname, dst, dtok):
                pass

        def rms_to_fm_gen(xb, xtok, NS, TP, gname, dst, dtok):
            T = NS * TP
            rms_stats(xb, xtok, NS, TP)
            for s in range(min(NS, 2)):
                ts("dve", ntm[s % 2][0:TP, :], xb[0:TP, s, :], stat[0:TP, 4 + s:5 + s], None, ALU.mult, None,
                   [xtok + "#%d" % s, "stat#rs"], ["ntm%d" % (s % 2)])
            yield
            for s in range(NS):
                nb = ntm[s % 2]
                ntk = "ntm%d" % (s % 2)
                if s >= 2:
                    ts("dve", nb[0:TP, :], xb[0:TP, s, :], stat[0:TP, 4 + s:5 + s], None, ALU.mult, None,
                       [xtok + "#%d" % s, "stat#rs"], [ntk])
                for kc in range(8):
                    b = kc // 2
                    o = (kc % 2) * 512 + s * TP
                    tr(psb[b][:, o:o + TP], nb[0:TP, kc * 128:(kc + 1) * 128], identb[0:TP, 0:TP],
                       [ntk, "identb"], ["ps%d#t%d" % (b, kc % 2)])
            for kc in range(8):
                b = kc // 2
                o = (kc % 2) * 512
                if b % 2 == 0:
                    act(dst[:, kc, 0:T], psb[b][:, o:o + T], AF.Identity, ["ps%d#t%d" % (b, kc % 2)] + VPR, [dtok],
                        scale=vcol(gname, kc))
                else:
                    ts("dve", dst[:, kc, 0:T], psb[b][:, o:o + T], vcol(gname, kc), None, ALU.mult, None,
                       ["ps%d#t%d" % (b, kc % 2)] + VPR, [dtok])

        def tm_to_fm(srcf, stok, NS, TP, dst, dtok):
            T = NS * TP
            for s in range(NS):
                for kc in range(8):
                    b = kc // 2
                    o = (kc % 2) * 512 + s * TP
                    tr(psb[b][:, o:o + TP], srcf(s)[:, kc * 128:(kc + 1) * 128], identb[0:TP, 0:TP],
                       stok(s) + ["identb"], ["ps%d#t%d" % (b, kc % 2)])
            for b in range(4):
                if b % 2 == 0:
                    act(dst[:, 2 * b:2 * b + 2, 0:T], psb[b][:, :].rearrange("p (a t) -> p a t", a=2)[:, :, 0:T], AF.Copy,
                        ["ps%d" % b], [dtok])
                else:
                    cp("dve", dst[:, 2 * b:2 * b + 2, 0:T], psb[b][:, :].rearrange("p (a t) -> p a t", a=2)[:, :, 0:T],
                       ["ps%d" % b], [dtok])

        def proj_fm(src, stok, wt, wtok, ncols, T, evac, nk=8):
            for m in range(ncols // 128):
                b = mmbank()
                for kc in range(nk):
                    mm(ps[b][:, 0:T], wt[:, kc, m * 128:(m + 1) * 128], src[:, kc, 0:T], [stok, wtok], ["ps%d" % b],
                       start=(kc == 0), stop=(kc == nk - 1))
                evac(m, ps[b][:, 0:T], "ps%d" % b)

        def proj_tm(src, stok, wt, wtok, ncols, NS, TP, evac, nk=8):
            for s in range(NS):
                b = mmbank()
                for kc in range(nk):
                    mm(ps[b][0:TP, 0:ncols], src[:, kc, s * TP:(s + 1) * TP], wt[:, kc, 0:ncols], [stok, wtok], ["ps%d" % b],
                       start=(kc == 0), stop=(kc == nk - 1))
                evac(s, ps[b][0:TP, 0:ncols], "ps%d" % b)

        def resid_proj(cfg, src, stok, wname):
            NS, TP = cfg["NS"], cfg["TP"]
            xb, xtok = cfg["xb"], cfg["xtok"]
            for half in range(2):
                wt, wk = wload(wname, 0, 8, half * 512, 512)

                def ev(s, pv, ptok, half=half):
                    dst = xb[0:TP, s, half * 512:(half + 1) * 512]
                    tt("dve", dst, pv, dst, ALU.add, [ptok, xtok + "#%d" % s], [xtok + "#%d" % s])
                proj_tm(src, stok, wt, wk, 512, NS, TP, ev)

        def hgrn_mixer(cfg, hq, hE, hi, hsg):
            T, NS, TP, C = cfg["T"], cfg["NS"], cfg["TP"], cfg["C"]
            NCH = T // C
            CS = TP // C
            qg = aalloc("qg", (4, T), BF16)
            kg = aalloc("kg", (4, T), BF16)
            kd_off = bump[0]
            kd = aalloc("kd", (4, T), BF16)
            kdtm = aalloc("kdtm", (NS, 512), BF16)
            otmp = [None, None]
            ebl = aalloc("ebl", (4, NCH), F32)
            hst = aalloc("hst", (1, 16), F32)
            tmps_off = bump[0]
            tmps = [[aalloc("h%s%d" % (n, i), (1, T), F32) for n in ("sg", "lf", "kf", "e1", "e2")] for i in range(1)]
            tmps = tmps + tmps
            end_off = bump[0]
            if T == 512:
                bump[0] = kd_off
                attm = [aalloc("attm%d" % i, (4, 128), BF16) for i in range(2)]
                otmp[0] = aalloc("otmp0", (4, 128), F32)
                otmp[1] = otmp[0]
                assert bump[0] <= kd_off + 4 * T * 2
                bump[0] = end_off
                attm = attm + [aalloc("attm%d" % i, (4, 128), BF16) for i in range(2, NS)]
                otmp[1] = otmp[0]
            else:
                attm = [aalloc("attm%d" % i, (4, 128), BF16) for i in range(NS)]
                otmp[0] = aalloc("otmp0", (4, 128), F32)
                otmp[1] = otmp[0]
            rm = rmask[C]
            if T == 512:
                bump_keep = bump[0]
                bump[0] = tmps_off
                dg = [aalloc("hdg%d" % i, (4, 128), F32) for i in range(2)]
                assert bump[0] <= end_off
                bump[0] = bump_keep
            else:
                dg = [aalloc("hdg%d" % i, (4, 128), F32) for i in range(2)]
            yield
            for h in range(4):
                S.phase = "HGprep"
                sg, lf, kf, e1, e2 = [t[:, 0, :] for t in tmps[h % 2]]
                bb = lf
                k_ = lambda n, h=h: "h%s%d" % (n, 0)
                sigmoid_from_exp(hE[:, h, :], "hE#%d" % h, out=sg, otok=k_("sg"))
                act(lf, sg, AF.Ln, [k_("sg")] + VPR, [k_("lf")], scale=vcol("oml", h), bias=vcol("lb", h))
                ts("pool", kf, sg, vcol("noml", h), vcol("oml", h), ALU.mult, ALU.add, [k_("sg")] + VPR, [k_("kf")])
                S.i("dve", "tensor_tensor_scan", [k_("lf"), "rmask%d" % C], [k_("lf")], out=bb, data0=rm[:, 0:T], data1=lf,
                    initial=0.0, op0=ALU.mult, op1=ALU.add)
                act(e1, bb, AF.Exp, [k_("lf")], [k_("e1")])
                act(e2, bb, AF.Exp, [k_("lf")], [k_("e2")], scale=-1.0)
                tt("dve", qg[:, h, :], hq[:, h, :], e1, ALU.mult, ["hq#%d" % h, k_("e1")], ["qg#%d" % h])
                tt("pool", kf, kf, e2, ALU.mult, [k_("kf"), k_("e2")], [k_("kf")])
                cp("pool", kg[:, h, :], kf, [k_("kf")], ["kg#%d" % h])
                cp("dve", ebl[:, h, :], e1.rearrange("p (n c) -> p n c", c=C)[:, :, C - 1], [k_("e1")], ["ebl#%d" % h])
                tt("dve", kd[:, h, :].rearrange("p (n c) -> p n c", c=C), kf.rearrange("p (n c) -> p n c", c=C),
                   ebl[:, h, :].unsqueeze(2).to_broadcast([128, NCH, C]), ALU.mult, [k_("kf"), "ebl#%d" % h], ["kd#%d" % h])
                yield
            S.phase = "HG"
            for s in range(NS):
                b = 6 + (s % 2)
                for h in range(4):
                    tr(psb[b][0:TP, h * 128:(h + 1) * 128], kd[:, h, s * TP:(s + 1) * TP], identb[:, :],
                       ["kd#%d" % h, "identb"], ["ps%d" % b])
                cp("act", kdtm[0:TP, s, :], psb[b][0:TP, 0:512], ["ps%d" % b], ["kdtm#%d" % s])
            mname = "hg%d" % C
            for s in range(NS):
                bA, bO = 4 + (s % 2), s
                am, amk = attm[s], "attm%d" % s
                tsl = slice(s * TP, (s + 1) * TP)
                for h in range(4):
                    mm(ps[bA][0:TP, h * 128:h * 128 + TP], kg[:, h, tsl], qg[:, h, tsl], ["kg#%d" % h, "qg#%d" % h],
                       ["ps%d" % bA])
                tt("dve", am[0:TP, :, 0:TP], ps[bA][0:TP, :].rearrange("p (h t) -> p h t", h=4)[:, :, 0:TP],
                   mk[mname][0:TP, 0:TP].unsqueeze(1).to_broadcast([TP, 4, TP]), ALU.mult, ["ps%d" % bA, "m_" + mname], [amk])
                for h in range(4):
                    hc = slice(h * 128, (h + 1) * 128)
                    mm(ps[bO][0:TP, hc], am[0:TP, h, 0:TP], hi[0:TP, s, hc], [amk, "hi#%d" % s], ["ps%d" % bO],
                       start=(h == 0), stop=False, skip_group_check=True)
            for s in range(NS):
                bO = s
                for j in range(CS):
                    ch = s * CS + j
                    Sf, Sb, Stok = cfg["S_hg"](ch)
                    rows = slice(j * C, (j + 1) * C)
                    tch = slice(ch * C, (ch + 1) * C)
                    bU = 6 + (ch % 2)
                    for h in range(4):
                        hc = slice(h * 128, (h + 1) * 128)
                        mm(ps[bO][rows, hc], qg[:, h, tch], Sb[:, h, :], ["qg#%d" % h, Stok + "b#%d" % h], ["ps%d" % bO],
                           start=False, stop=(j == CS - 1), skip_group_check=True)
                    dgs = dg[ch % 2]
                    for h in range(4):
                        ts("pool", dgs[:, h, :], identf[:, :], ebl[:, h, ch:ch + 1], 1.0, ALU.mult, ALU.mult,
                           ["identf", "ebl#%d" % h], ["hdg%d#%d" % (ch % 2, h)])
                    for h in range(4):
                        hc = slice(h * 128, (h + 1) * 128)
                        mm(ps[bU][:, hc], kdtm[rows, s, hc], hi[rows, s, hc], ["kdtm#%d" % s, "hi#%d" % s], ["ps%d" % bU],
                           start=True, stop=False, skip_group_check=True)
                        mm(ps[bU][:, hc], dgs[:, h, :], Sf[:, h, :], ["hdg%d#%d" % (ch % 2, h), Stok + "f#%d" % h], ["ps%d" % bU],
                           start=False, stop=True, skip_group_check=True)
                    act(Sb[:, :, :].rearrange("p h v -> p (h v)"), ps[bU][:, 0:512], AF.Copy, ["ps%d" % bU],
                        [Stok + "b#%d" % h for h in range(4)])
                    cp("dve", Sf[:, :, :].rearrange("p h v -> p (h v)"), ps[bU][:, 0:512], ["ps%d" % bU],
                       [Stok + "f#%d" % h for h in range(4)])
            for s in range(NS):
                bO = s
                ot, otk = otmp[0], "otmp0"
                for h in range(4):
                    act(ot[0:TP, h, :], ps[bO][0:TP, h * 128:(h + 1) * 128], AF.Square, ["ps%d" % bO], [otk, "hst#s%d" % h],
                        accum_out=hst[0:TP, 0, h:h + 1])
                rsqrt_chain(hst[0:TP, 0, 4:8], hst[0:TP, 0, 0:4], ["hst#s%d" % h for h in range(4)], "hst#r", 1.0 / 128, RMS_EPS)
                tt("dve", ot[0:TP, :, :], ps[bO][0:TP, :].rearrange("p (h v) -> p h v", h=4),
                   hst[0:TP, 0, 4:8].unsqueeze(2).to_broadcast([TP, 4, 128]), ALU.mult, ["ps%d" % bO, "hst#r", otk], [otk])
                tt("pool", mixtm[0:TP, s, 0:512], ot[0:TP, :, :].rearrange("p h v -> p (h v)"), hsg[0:TP, s, :], ALU.mult,
                   [otk, "hsg#%d" % s], ["mixtm#h%d" % s])
            yield

        def rwkv_mixer(cfg, rkv, lora, rwg):
            T, NS, TP, C = cfg["T"], cfg["NS"], cfg["TP"], cfg["C"]
            NCH = T // C
            CS = TP // C
            HC = 2 * C
            L = {64: 5, 32: 4}[C]
            bump[0] = cfg["mark1"]
            tha = aalloc("tha", (1, T), BF16)
            sgd = aalloc("sgd", (1, T), BF16)
            sqbs = [aalloc("sqb%d" % q, (1, T), BF16) for q in range(2)]
            rkbs = [aalloc("rkb%d" % q, (1, T), BF16) for q in range(2)]
            ysq_ = aalloc("ysq_", (1, 512), F32)
            cen_ = aalloc("cen_", (1, 512), F32)
            wc = aalloc("wc", (4, NCH), F32)
            bon = aalloc("bon", (NS, 8), F32)
            gst = aalloc("gst", (1, 32), F32)
            yrw = aalloc("yrw", (NS, 512), F32)
            kt_ = aalloc("kt_", (4, T), BF16)
            rt_ = aalloc("rt_", (4, T), BF16)
            bt_ = aalloc("bt_", (4, T), BF16)
            ktl = aalloc("ktl", (4, T), BF16)
            ft_off = bump[0]
            ftset = [[aalloc("rt%d_%d" % (q, i), (1, T), F32)[:, 0, :] for i in range(6)] for q in range(2)]
            fkset = [["rt%d_%d" % (q, i) for i in range(6)] for q in range(2)]
            ft, fk = ftset[0], fkset[0]
            NSET = 4
            cs_ = []
            end_rw = bump[0]
            bump[0] = ft_off
            for i in range(NSET):
                d = dict(tm=aalloc("c%dtm" % i, (5, 128), BF16), N=aalloc("c%dN" % i, (1, 128), BF16),
                         NTs=aalloc("c%dNTs" % i, (1, 256), BF16), AkT=aalloc("c%dAkT" % i, (1, 256), BF16),
                         X=[aalloc("c%dX%d" % (i, j), (2, 128), BF16) for j in range(2)],
                         TT=[aalloc("c%dTT%d" % (i, j), (1, 128), BF16) for j in range(2)],
                         WU=aalloc("c%dWU" % i, (2, 128), BF16), Mp=aalloc("c%dMp" % i, (1, 128), BF16),
                         GT=aalloc("c%dGT" % i, (1, 64), BF16))
                cs_.append(d)
            bump[0] = max(bump[0], end_rw)
            rm = rmask[C]
            t0 = ft[0]
            act(t0[0:64, :], lora[0:64, 0, :], AF.Exp, ["lora#0"], [fk[0]], scale=2.0)
            sigmoid_from_exp(t0[0:64, :], fk[0])
            ts("dve", tha[0:64, 0, :], t0[0:64, :], -2.0, 1.0, ALU.mult, ALU.add, [fk[0]], ["tha#0"])
            cp("pool", tha[64:128, 0, :], lora[64:128, 0, :], ["lora#0"], ["tha#1"])
            t1 = ft[1]
            act(t1, lora[:, 1, :], AF.Exp, ["lora#1"], [fk[1]], scale=-1.0)
            sigmoid_from_exp(t1, fk[1], out=sgd[:, 0, :], otok="sgd")
            for s in range(NS):
                b = mmbank()
                mm(ps[b][0:TP, :], sgd[:, 0, s * TP:(s + 1) * TP], wlo[:, 1, :], ["sgd", "wlo#2"], ["ps%d" % b])
                cp("act", rwg[0:TP, s, :], ps[b][0:TP, :], ["ps%d" % b], ["rwg#%d" % s])
            def prep_gen(hp):
                hpc = slice(hp * 128, (hp + 1) * 128)
                xr, xk = rkv[:, hp, :], rkv[:, 4 + hp, :]
                rtok, ktok = "rkv#%d" % hp, "rkv#%d" % (4 + hp)
                sz, a_, kk, rn, k2, ex = ftset[hp % 2]
                ksz, ka_, kkk, krn, kk2, kex = fkset[hp % 2]
                cs, kcs = a_, ka_
                sqb, rkb = sqbs[hp % 2], rkbs[hp % 2]
                sqk, rkk = "sqb%d" % (hp % 2), "rkb%d" % (hp % 2)
                b = mmbank()
                mm(ps[b][:, 0:T], wlo[0:64, 0, hpc], tha[0:64, 0, :], ["wlo#0", "tha#0"], ["ps%d" % b])
                act(sz, ps[b][:, 0:T], AF.Exp, ["ps%d" % b] + VPR, [ksz], scale=-1.0, bias=vcol("nw0", hp))
                sigmoid_from_exp(sz, ksz)
                yield
                b = mmbank()
                mm(ps[b][:, 0:T], wlo[64:128, 0, hpc], tha[64:128, 0, :], ["wlo#1", "tha#1"], ["ps%d" % b])
                act(a_, ps[b][:, 0:T], AF.Exp, ["ps%d" % b] + VPR, [ka_], scale=-1.0, bias=vcol("na0", hp))
                sigmoid_from_exp(a_, ka_)
                yield
                ts("dve", kk, xk, vcol("kk", hp), None, ALU.mult, None, [ktok] + VPR, [kkk])
                act(sqb[:, 0, :], kk, AF.Square, [kkk], [sqk])
                yield
                b = mmbank()
                mm(ps[b][:, 0:T], e64[:, :], sqb[:, 0, :], ["e64", sqk], ["ps%d" % b])
                ts("dve", rn, ps[b][:, 0:T], 1e-24, None, ALU.max, None, ["ps%d" % b], [krn])
                act(rn, rn, AF.Ln, [krn], [krn])
                act(rn, rn, AF.Exp, [krn], [krn], scale=-0.5)
                yield
                tt("dve", kk, kk, rn, ALU.mult, [kkk, krn], [kkk])
                ts("pool", k2, a_, vcol("ka", hp), vcol("omka", hp), ALU.mult, ALU.add, [ka_] + VPR, [kk2])
                tt("dve", k2, k2, xk, ALU.mult, [kk2, ktok], [kk2])
                yield
                stt(rkb[:, 0, :], k2, vcol("rk", hp), xr, ALU.mult, ALU.mult, [kk2, rtok] + VPR, [rkk])
                tt("pool", rn, kk, a_, ALU.mult, [kkk, ka_], [krn])
                yield
                S.i("dve", "tensor_tensor_scan", [ksz, "rmask%d" % C], [kcs], out=cs, data0=rm[:, 0:T], data1=sz,
                    initial=0.0, op0=ALU.mult, op1=ALU.add)
                act(ex, cs, AF.Exp, [kcs], [kex], scale=-DECAY_K)
                yield
                cp("act", wc[:, hp, :], ex.rearrange("p (n c) -> p n c", c=C)[:, :, C - 1], [kex], ["wc#%d" % hp])
                tt("dve", rt_[:, hp, :], xr, ex, ALU.mult, [rtok, kex], ["rt_#%d" % hp])
                act(ex, cs, AF.Exp, [kcs], [kex], scale=DECAY_K)
                yield
                tt("dve", bt_[:, hp, :], rn, ex, ALU.mult, [krn, kex], ["bt_#%d" % hp])
                tt("pool", ktl[:, hp, :], k2, ex, ALU.mult, [kk2, kex], ["ktl#%d" % hp])
                tt("dve", sz, cs, sz, ALU.subtract, [kcs, ksz], [ksz])
                yield
                act(ex, sz, AF.Exp, [ksz], [kex], scale=-DECAY_K)
                tt("dve", kt_[:, hp, :], kk, ex, ALU.mult, [kkk, kex], ["kt_#%d" % hp])
                yield
                for s in range(NS):
                    bb_ = mmbank()
                    mm(ps[bb_][0:TP, 0:2], rkb[:, 0, s * TP:(s + 1) * TP], ehead[:, :], [rkk, "ehead"], ["ps%d" % bb_])
                    cp("act", bon[0:TP, s, 2 * hp:2 * hp + 2], ps[bb_][0:TP, 0:2], ["ps%d" % bb_], ["bon#%d_%d" % (s, hp)])

            for pair in ((0, 1), (2, 3)):
                gens = [prep_gen(h_) for h_ in pair]
                while gens:
                    for g_ in list(gens):
                        try:
                            next(g_)
                        except StopIteration:
                            gens.remove(g_)
            dbg_store("kt_", kt_[:, :, :], ["kt_"])
            dbg_store("rt_", rt_[:, :, :], ["rt_"])
            dbg_store("bt_", bt_[:, :, :], ["bt_"])
            dbg_store("ktl", ktl[:, :, :], ["ktl"])
            dbg_store("wc", wc[:, :, :], ["wc"])
            o_w, o_b = VB_COLS["gnw"], VB_COLS["gnb"]
            ysq, cen = ysq_[:, 0, :], cen_[:, 0, :]
            fk = ["ysq_", "cen_"]
            def rw_post(s):
                S.phase = "RWpost"
                y3 = yrw[0:TP, s, :].rearrange("p (h v) -> p h v", h=8)
                yk = "yrw#%d" % s
                S.i("dve", "tensor_reduce", [yk], ["gst#s1"], out=gst[0:TP, 0, 0:8], in_=y3, axis=mybir.AxisListType.X,
                    op=ALU.add)
                act(ysq[0:TP, 0:512], yrw[0:TP, s, :], AF.Square, [yk], [fk[0]])
                S.i("dve", "tensor_reduce", [fk[0]], ["gst#s2"], out=gst[0:TP, 0, 8:16],
                    in_=ysq[0:TP, 0:512].rearrange("p (h v) -> p h v", h=8), axis=mybir.AxisListType.X, op=ALU.add)
                ts("dve", gst[0:TP, 0, 0:8], gst[0:TP, 0, 0:8], 1.0 / 64, None, ALU.mult, None, ["gst#s1"], ["gst#s1"])
                tt("dve", gst[0:TP, 0, 16:24], gst[0:TP, 0, 0:8], gst[0:TP, 0, 0:8], ALU.mult, ["gst#s1"], ["gst#m2"])
                stt(gst[0:TP, 0, 8:16], gst[0:TP, 0, 8:16], 1.0 / 64, gst[0:TP, 0, 16:24], ALU.mult, ALU.subtract,
                    ["gst#s2", "gst#m2"], ["gst#s2"])
                rsqrt_chain(gst[0:TP, 0, 8:16], gst[0:TP, 0, 8:16], ["gst#s2"], "gst#s2", 1.0, GN_EPS)
                c3 = cen[0:TP, 0:512].rearrange("p (h v) -> p h v", h=8)
                tt("dve", c3, y3, gst[0:TP, 0, 0:8].unsqueeze(2).to_broadcast([TP, 8, 64]), ALU.subtract, [yk, "gst#s1"], [fk[1]])
                tt("dve", c3, c3, gst[0:TP, 0, 8:16].unsqueeze(2).to_broadcast([TP, 8, 64]), ALU.mult, [fk[1], "gst#s2"], [fk[1]])
                tt("pool", cen[0:TP, 0:512], cen[0:TP, 0:512], vbb[0:TP, o_w:o_w + 512], ALU.mult, [fk[1], "vbb"], [fk[1]])
                tt("pool", cen[0:TP, 0:512], cen[0:TP, 0:512], vbb[0:TP, o_b:o_b + 512], ALU.add, [fk[1], "vbb"], [fk[1]])
                bV = mmbank()
                for hp in range(4):
                    tr(psb[bV][0:TP, hp * 128:(hp + 1) * 128], rkv[:, 8 + hp, s * TP:(s + 1) * TP], identb[:, :],
                       ["rkv#%d" % (8 + hp), "identb"], ["ps%d" % bV])
                tt("dve", ysq[0:TP, 0:512].rearrange("p (h v) -> p h v", h=8),
                   psb[bV][0:TP, 0:512].rearrange("p (h v) -> p h v", h=8),
                   bon[0:TP, s, :].unsqueeze(2).to_broadcast([TP, 8, 64]), ALU.mult,
                   ["ps%d" % bV] + ["bon#%d_%d" % (s, hp) for hp in range(4)], [fk[0]])
                tt("pool", cen[0:TP, 0:512], cen[0:TP, 0:512], ysq[0:TP, 0:512], ALU.add, [fk[0], fk[1]], [fk[1]])
                tt("dve", mixtm[0:TP, s, 512:1024], cen[0:TP, 0:512], rwg[0:TP, s, :], ALU.mult, [fk[1], "rwg#%d" % s],
                   ["mixtm#r%d" % s])


            S.phase = "RWchunk"
            for i in range(NSET):
                for nm in ("tm",):
                    S.i("pool", "memset", [], ["c%d%s" % (i, nm)], ap=cs_[i][nm][:], constant=0.0)
            hr = lambda h: slice(h * 64, (h + 1) * 64)
            trw = lambda h: slice(h * C, (h + 1) * C)
            m1, m2, m3 = mk["a1_%d" % C], mk["a2_%d" % C], mk["a3_%d" % C]
            for ch in range(NCH):
                csl = slice(ch * C, (ch + 1) * C)
                s, j = ch // CS, ch % CS
                Hf, Hb, Htok = cfg["H_rw"](ch)
                st = [cs_[hp % NSET] for hp in range(4)]
                pre = ["c%d" % (hp % NSET) for hp in range(4)]
                for hp in range(4):
                    d, p_ = st[hp], pre[hp]
                    bA, bB = 2 * hp, 2 * hp + 1
                    arrs = [(kt_[:, hp, :], "kt_#%d" % hp), (rkv[:, 8 + hp, :], "rkv#%d" % (8 + hp)),
                            (bt_[:, hp, :], "bt_#%d" % hp), (ktl[:, hp, :], "ktl#%d" % hp)]
                    for a, (arr, atok) in enumerate(arrs):
                        for h in range(2):
                            tr(psb[bA][trw(h), a * 128 + h * 64:a * 128 + (h + 1) * 64], arr[hr(h), csl],
                               identb[hr(h), hr(h)], [atok, "identb"], ["ps%d" % bA])
                    for h in range(2):
                        src = psb[bA][trw(h), 0:512].rearrange("p (a c) -> p a c", a=4)[:, :, h * 64:(h + 1) * 64]
                        dst = d["tm"][trw(h), :, :]
                        eng = "act" if h == 0 else "dve"
                        cp(eng, dst[:, 0:1, h * 64:(h + 1) * 64], src[:, 0:1, :], ["ps%d" % bA], [p_ + "tm#a"])
                        cp(eng, dst[:, 2:5, h * 64:(h + 1) * 64], src[:, 1:4, :], ["ps%d" % bA], [p_ + "tm#b"])
                    for h in range(2):
                        kth, bth = kt_[hr(h), hp, csl], bt_[hr(h), hp, csl]
                        rth, klh = rt_[hr(h), hp, csl], ktl[hr(h), hp, csl]
                        rw_ = ["kt_#%d" % hp, "bt_#%d" % hp, "rt_#%d" % hp, "ktl#%d" % hp]
                        mm(ps[bB][trw(h), h * C:(h + 1) * C], kth, bth, rw_, ["ps%d#m1" % bB])
                        mm(ps[bB][trw(h), 128 + h * C:128 + (h + 1) * C], bth, kth, rw_, ["ps%d#m2" % bB])
                        mm(ps[bB][trw(h), 128 + 2 * C:128 + 3 * C], bth, rth, rw_, ["ps%d#m2" % bB])
                        mm(ps[bB][trw(h), 320 + h * C:320 + (h + 1) * C], klh, kth, rw_, ["ps%d#m3" % bB])
                        mm(ps[bB][trw(h), 320 + 2 * C:320 + 3 * C], klh, rth, rw_, ["ps%d#m3" % bB])
                    tt("dve", d["N"][0:HC, 0, 0:HC], ps[bB][0:HC, 0:HC], m1[0:HC, 0:HC], ALU.mult,
                       ["ps%d#m1" % bB, "m_a1_%d" % C], [p_ + "N"])
                    tt("dve", d["NTs"][0:HC, 0, 0:3 * C], ps[bB][0:HC, 128:128 + 3 * C], m2[0:HC, 0:3 * C], ALU.mult,
                       ["ps%d#m2" % bB, "m_a2_%d" % C], [p_ + "NTs"])
                    tt("dve", d["AkT"][0:HC, 0, 0:3 * C], ps[bB][0:HC, 320:320 + 3 * C], m3[0:HC, 0:3 * C], ALU.mult,
                       ["ps%d#m3" % bB, "m_a3_%d" % C], [p_ + "AkT"])
                for hp in range(4):
                    d, p_ = st[hp], pre[hp]
                    bA = 2 * hp
                    mm(ps[bA][0:HC, 0:128], d["AkT"][0:HC, 0, 0:HC], d["tm"][0:HC, 2, :], [p_ + "AkT", p_ + "tm#b"], ["ps%d" % bA])
                    cp("act", d["tm"][0:HC, 1, :], ps[bA][0:HC, 0:128], ["ps%d" % bA], [p_ + "tm#c"])
                    tt("pool", d["TT"][0][0:HC, 0, 0:HC], d["NTs"][0:HC, 0, 0:HC], identb[0:HC, 0:HC], ALU.add,
                       [p_ + "NTs", "identb"], [p_ + "TT0"])
                for lv in range(1, L + 1):
                    for hp in range(4):
                        d, p_ = st[hp], pre[hp]
                        bX = 2 * hp + (lv % 2)
                        if lv == 1:
                            Xp, XTp = d["N"][0:HC, 0, 0:HC], d["NTs"][0:HC, 0, 0:HC]
                            rp = [p_ + "N", p_ + "NTs"]
                        else:
                            Xp, XTp = d["X"][(lv - 1) % 2][0:HC, 0, 0:HC], d["X"][(lv - 1) % 2][0:HC, 1, 0:HC]
                            rp = [p_ + "X%d" % ((lv - 1) % 2)]
                        Xn, xnk = d["X"][lv % 2], p_ + "X%d" % (lv % 2)
                        mm(ps[bX][0:HC, 0:HC], XTp, Xp, rp, ["ps%d" % bX])
                        if lv < L:
                            mm(ps[bX][0:HC, 128:128 + HC], Xp, XTp, rp, ["ps%d" % bX])
                            src = ps[bX][0:HC, 0:256].rearrange("p (a b) -> p a b", a=2)[:, :, 0:HC]
                            cp("act" if hp % 2 == 0 else "dve", Xn[0:HC, :, 0:HC], src, ["ps%d" % bX], [xnk])
                        else:
                            cp("act" if hp % 2 == 0 else "dve", Xn[0:HC, 0, 0:HC], ps[bX][0:HC, 0:HC], ["ps%d" % bX], [xnk])
                    for hp in range(4):
                        d, p_ = st[hp], pre[hp]
                        bT = 2 * hp + ((lv + 1) % 2)
                        Xn, xnk = d["X"][lv % 2], p_ + "X%d" % (lv % 2)
                        TTo, TTn = d["TT"][(lv - 1) % 2], d["TT"][lv % 2]
                        tko, tkn = p_ + "TT%d" % ((lv - 1) % 2), p_ + "TT%d" % (lv % 2)
                        mm(ps[bT][0:HC, 256:256 + HC], identb[0:HC, 0:HC], TTo[0:HC, 0, 0:HC], ["identb", tko], ["ps%d" % bT],
                           start=True, stop=False, skip_group_check=True)
                        mm(ps[bT][0:HC, 256:256 + HC], Xn[0:HC, 0, 0:HC], TTo[0:HC, 0, 0:HC], [xnk, tko], ["ps%d" % bT],
                           start=False, stop=True, skip_group_check=True)
                        cp("act" if hp % 2 == 1 else "dve", TTn[0:HC, 0, 0:HC], ps[bT][0:HC, 256:256 + HC], ["ps%d" % bT], [tkn])
                for hp in range(4):
                    d, p_ = st[hp], pre[hp]
                    bA = 2 * hp
                    TTL, tkl = d["TT"][L % 2], p_ + "TT%d" % (L % 2)
                    mm(ps[bA][0:HC, 0:256], TTL[0:HC, 0, 0:HC], d["tm"][0:HC, 0:2, :].rearrange("p a b -> p (a b)"),
                       [tkl, p_ + "tm#a", p_ + "tm#c"], ["ps%d" % bA])
                    act(d["WU"][0:HC, :, :].rearrange("p a b -> p (a b)"), ps[bA][0:HC, 0:256], AF.Copy, ["ps%d" % bA],
                        [p_ + "WU"], scale=-1.0)
                for hp in range(4):
                    d, p_ = st[hp], pre[hp]
                    bB = 2 * hp + 1
                    mm(ps[bB][:, 0:128], d["WU"][0:HC, 0, :], d["tm"][0:HC, 3, :], [p_ + "WU", p_ + "tm#b"], ["ps%d#m1" % bB])
                    mm(ps[bB][:, 128:128 + C], d["WU"][0:HC, 0, :], d["NTs"][0:HC, 0, 2 * C:3 * C], [p_ + "WU", p_ + "NTs"],
                       ["ps%d#m2" % bB])
                    tt("dve", d["Mp"][:, 0, :], ps[bB][:, 0:128], identb[:, :], ALU.add, ["ps%d#m1" % bB, "identb"], [p_ + "Mp"])
                    tt("dve", d["GT"][:, 0, 0:C], ps[bB][:, 128:128 + C], rt_[:, hp, csl], ALU.add,
                       ["ps%d#m2" % bB, "rt_#%d" % hp], [p_ + "GT"])
                for hp in range(4):
                    d, p_ = st[hp], pre[hp]
                    bA, bB = 2 * hp, 2 * hp + 1
                    hpc = slice(hp * 128, (hp + 1) * 128)
                    rows = slice(j * C, (j + 1) * C)
                    hbk, hfk = Htok + "b#%d" % hp, Htok + "f#%d" % hp
                    mm(ps[bA][rows, 256:384], d["NTs"][0:HC, 0, 2 * C:3 * C], d["WU"][0:HC, 1, :], [p_ + "NTs", p_ + "WU"],
                       ["ps%d" % bA], start=True, stop=False)
                    mm(ps[bA][rows, 256:384], d["AkT"][0:HC, 0, 2 * C:3 * C], d["tm"][0:HC, 2, :], [p_ + "AkT", p_ + "tm#b"],
                       ["ps%d" % bA], start=False, stop=False)
                    mm(ps[bA][rows, 256:384], d["GT"][:, 0, 0:C], Hb[:, hp, :], [p_ + "GT", hbk], ["ps%d" % bA],
                       start=False, stop=True)
                    cp("act", yrw[rows, s, hpc], ps[bA][rows, 256:384], ["ps%d" % bA], ["yrw#%d" % s])
                    mm(ps[bB][:, 320:448], d["tm"][0:HC, 3, :], d["WU"][0:HC, 1, :], [p_ + "tm#b", p_ + "WU"], ["ps%d#m3" % bB],
                       start=True, stop=False)
                    mm(ps[bB][:, 320:448], d["tm"][0:HC, 4, :], d["tm"][0:HC, 2, :], [p_ + "tm#b"], ["ps%d#m3" % bB],
                       start=False, stop=False)
                    mm(ps[bB][:, 320:448], d["Mp"][:, 0, :], Hb[:, hp, :], [p_ + "Mp", hbk], ["ps%d#m3" % bB],
                       start=False, stop=True)
                    if cfg["hf_chunks"] is None or ch in cfg["hf_chunks"]:
                        ts("dve", Hf[:, hp, :], ps[bB][:, 320:448], wc[:, hp, ch:ch + 1], None, ALU.mult, None,
                           ["ps%d#m3" % bB, "wc#%d" % hp], [hfk])
                    act(Hb[:, hp, :], ps[bB][:, 320:448], AF.Identity, ["ps%d#m3" % bB, "wc#%d" % hp], [hbk],
                        scale=wc[:, hp, ch:ch + 1])
                if j == CS - 1:
                    rw_post(s)
                    S.phase = "RWchunk"
            dbg_store("yrw", yrw[0:TP, :, :], ["yrw"])

        def cross_attn(cfg):
            T, NS, TP = cfg["T"], cfg["NS"], cfg["TP"]
            xb, xtok = cfg["xb"], cfg["xtok"]
            bump[0] = 0
            qfm = aalloc("qfm", (8, T), BF16)
            prT = aalloc("prT", (8, T), BF16)
            prf = [aalloc("prf%d" % i, (4, NMEM), F32) for i in range(2)]
            prb = [aalloc("prb%d" % i, (4, NMEM), BF16) for i in range(2)]
            cst = aalloc("cst", (2, 16), F32)
            if cfg["sample"]:
                smem = []
                for q in range(2):
                    mkf = aalloc("smk%d" % q, (8, NMEM), BF16)
                    mvt = aalloc("smv%d" % q, (2, D), BF16)
                    mkt = aalloc("smkt%d" % q, (2, D), BF16)
                    S.dma("pool", mvt[:, :, :], I["cv"][q].rearrange("(mc p) d -> p mc d", p=128), writes=["sm%dv" % q],
                          semkey="smv%d" % q)
                    S.dma("pool", mkt[:, :, :], I["ck"][q].rearrange("(mc p) d -> p mc d", p=128), writes=["smkt%d" % q],
                          semkey="smk%d" % q)
                    for dch in range(8):
                        b = mmbank()
                        for mc in range(2):
                            tr(psb[b][:, mc * 128:(mc + 1) * 128], mkt[:, mc, dch * 128:(dch + 1) * 128], identb[:, :],
                               ["smkt%d" % q, "identb"], ["ps%d" % b])
                        cp("act" if dch % 2 == 0 else "dve", mkf[:, dch, :], psb[b][:, 0:NMEM], ["ps%d" % b], ["sm%dk" % q])
                    smem.append((mkf, mvt, "sm%d" % q))
                cfg["mem"] = lambda mi: smem[mi]
            rms_to_fm(xb, xtok, NS, TP, "g_cross", fm[0], "fm0")
            for half in range(2):
                wt, wk = wload("w_cq", 0, 8, half * 512, 512)

                def evq(m, pv, ptok, half=half):
                    act(qfm[:, half * 4 + m, :], pv, AF.Copy, [ptok], ["qfm#%d" % (half * 4 + m)], scale=1.0 / 16.0)
                proj_fm(fm[0], "fm0", wt, wk, 512, T, evq)
            cstk = lambda s, n: "cst%d#%s" % (s % 2, n)

            def scores(s):
                tsl = slice(s * TP, (s + 1) * TP)
                b0 = 4 + 2 * (s % 2)
                for (rows, mi) in cfg["att_groups"](s):
                    mkfm_i, mvtm_i, mtok = cfg["mem"](mi)
                    for h in range(4):
                        b = b0 + (h // 2)
                        for dc in range(2):
                            mm(ps[b][rows, (h % 2) * 256:(h % 2) * 256 + NMEM],
                               qfm[:, 2 * h + dc, tsl.start + rows.start:tsl.start + rows.stop],
                               mkfm_i[:, 2 * h + dc, :], ["qfm#%d" % (2 * h + dc), mtok + "k"], ["ps%d" % b],
                               start=(dc == 0), stop=(dc == 1), skip_group_check=True)

                cs2 = cst[:, s % 2, :]
                for bi, hh in ((0, 0), (1, 2)):
                    b = b0 + bi
                    S.i("dve", "tensor_reduce", ["ps%d" % b], [cstk(s, "m%d" % bi)], out=cs2[0:TP, hh:hh + 2],
                        in_=ps[b][0:TP, :].rearrange("p (h m) -> p h m", h=2), axis=mybir.AxisListType.X, op=ALU.max)
                ts("dve", cs2[0:TP, 4:8], cs2[0:TP, 0:4], -1.0, None, ALU.mult, None, [cstk(s, "m0"), cstk(s, "m1")], [cstk(s, "n")])

            def softmax_T(s):
                tsl = slice(s * TP, (s + 1) * TP)
                b0 = 4 + 2 * (s % 2)
                pf_, pb_ = prf[s % 2], prb[s % 2]
                pfk, pbk = "prf%d" % (s % 2), "prb%d" % (s % 2)
                cs2 = cst[:, s % 2, :]
                for h in range(4):
                    b = b0 + (h // 2)
                    act(pf_[0:TP, h, :], ps[b][0:TP, (h % 2) * 256:(h % 2) * 256 + NMEM], AF.Exp, ["ps%d" % b, cstk(s, "n")],
                        [pfk + "#%d" % h, cstk(s, "s%d" % h)], bias=cs2[0:TP, 4 + h:5 + h], accum_out=cs2[0:TP, 8 + h:9 + h])
                recip(cs2[0:TP, 12:16], cs2[0:TP, 8:12], [cstk(s, "s%d" % h) for h in range(4)], [cstk(s, "r")])
                for h in range(4):
                    ts("dve", pb_[0:TP, h, :], pf_[0:TP, h, :], cs2[0:TP, 12 + h:13 + h], None, ALU.mult, None,
                       [pfk + "#%d" % h, cstk(s, "r")], [pbk + "#%d" % h])
                for h in range(4):
                    b = h
                    for mc in range(2):
                        tr(psb[b][:, mc * 128:mc * 128 + TP], pb_[0:TP, h, mc * 128:(mc + 1) * 128], identb[0:TP, 0:TP],
                           [pbk + "#%d" % h, "identb"], ["ps%d" % b])
                    src = psb[b][:, 0:256].rearrange("p (a t) -> p a t", a=2)[:, :, 0:TP]
                    cp("act" if h % 2 == 0 else "dve", prT[:, 2 * h:2 * h + 2, tsl], src, ["ps%d" % b], ["prT#%d_%d" % (h, s)])

            scores(0)
            for s in range(NS):
                if s + 1 < NS:
                    scores(s + 1)
                softmax_T(s)
            allpr = ["prT#%d_%d" % (h, s) for h in range(4) for s in range(NS)]
            for (csl2, mi) in cfg["att_full"]:
                mkfm_i, mvtm_i, mtok = cfg["mem"](mi)
                nr = csl2.stop - csl2.start
                for dch in range(8):
                    h = dch // 2
                    b = mmbank()
                    for mc in range(2):
                        mm(ps[b][:, 0:nr], mvtm_i[:, mc, dch * 128:(dch + 1) * 128], prT[:, 2 * h + mc, csl2],
                           [mtok + "v"] + allpr, ["ps%d" % b], start=(mc == 0), stop=(mc == 1))
                    cp("act" if dch % 2 == 0 else "dve", fm[1][:, dch, csl2], ps[b][:, 0:nr], ["ps%d" % b], ["fm1"])
            resid_proj(cfg, fm[1], "fm1", "w_co")

        def ffn(cfg):
            T, NS, TP = cfg["T"], cfg["NS"], cfg["TP"]
            xb, xtok = cfg["xb"], cfg["xtok"]
            bump[0] = 0
            actf = aalloc("actf", (22, T), BF16)
            h1s = [aalloc("h1s%d" % i, (1, T), F32) for i in range(2)]
            sgs = [aalloc("sgs%d" % i, (1, T), F32) for i in range(2)]
            rms_to_fm(xb, xtok, NS, TP, "g_ffn", fm[0], "fm0")
            cnt = {"n": 0}
            blocks = [(c0, min(512, DFF - c0)) for c0 in range(0, DFF, 512)]
            for (c0, ncols) in blocks:
                wt1, wk1 = wload("w_ff1", 0, 8, c0, ncols)
                wt3, wk3 = wload("w_ff3", 0, 8, c0, ncols)
                for m in range(ncols // 128):
                    fch = c0 // 128 + m
                    i = cnt["n"] % 2
                    cnt["n"] += 1
                    h1, sg_ = h1s[i][:, 0, :], sgs[i][:, 0, :]
                    hk, sk = "h1s%d" % i, "sgs%d" % i
                    b1 = mmbank()
                    for kc in range(8):
                        mm(ps[b1][:, 0:T], wt1[:, kc, m * 128:(m + 1) * 128], fm[0][:, kc, 0:T], ["fm0", wk1], ["ps%d" % b1],
                           start=(kc == 0), stop=(kc == 7))
                    b3 = mmbank()
                    for kc in range(8):
                        mm(ps[b3][:, 0:T], wt3[:, kc, m * 128:(m + 1) * 128], fm[0][:, kc, 0:T], ["fm0", wk3], ["ps%d" % b3],
                           start=(kc == 0), stop=(kc == 7))
                    act(sg_, ps[b1][:, 0:T], AF.Exp, ["ps%d" % b1], [sk], scale=-1.0)
                    sigmoid_from_exp(sg_, sk)
                    tt("dve", h1, ps[b1][:, 0:T], sg_, ALU.mult, ["ps%d" % b1, sk], [hk])
                    tt("dve", actf[:, fch, :], ps[b3][:, 0:T], h1, ALU.mult, ["ps%d" % b3, hk], ["actf#%d" % fch])
            agen = None
            if cfg.get("next") is not None:
                cfg["next"]["a_phase_back"] = "F2"
                agen = phase_a(cfg["next"])
                next(agen)
                cfg["next"]["a_done"] = True
            S.phase = "F2"
            for half in range(2):
                banks = [mmbank() for _ in range(NS)]
                kgroups = [(0, 8), (8, 8), (16, 6)]
                for gi, (k0, nk) in enumerate(kgroups):
                    wt, wk = wload("w_ff2", k0, nk, half * 512, 512)
                    for s in range(NS):
                        b = banks[s]
                        for kk_ in range(nk):
                            kc = k0 + kk_
                            mm(ps[b][0:TP, :], actf[:, kc, s * TP:(s + 1) * TP], wt[:, kk_, :], ["actf#%d" % kc, wk], ["ps%d" % b],
                               start=(kc == 0), stop=(kc == 21))
                if half == 0 and agen is not None:
                    for _ in agen:
                        pass
                    S.phase = "F2"
                for s in range(NS):
                    b = banks[s]
                    dst = xb[0:TP, s, half * 512:(half + 1) * 512]
                    tt("dve", dst, ps[b][0:TP, :], dst, ALU.add, ["ps%d" % b, xtok + "#%d" % s], [xtok + "#%d" % s])
            S.phase = "FIN"
            rms_stats(xb, xtok, NS, TP)
            o_nf = VB_COLS["nf"]
            for s in range(NS):
                stt(xb[0:TP, s, :], xb[0:TP, s, :], stat[0:TP, 4 + s:5 + s], vbb[0:TP, o_nf:o_nf + D], ALU.mult, ALU.mult,
                    [xtok + "#%d" % s, "stat#rs", "vbb"], [xtok + "#%d" % s])
                S.dma("act", cfg["ydst"][s * TP:(s + 1) * TP, :], xb[0:TP, s, :], reads=[xtok + "#%d" % s],
                      semkey=xtok + "_st", store=True)

        def phase_a(cfg):
            NS, TP = cfg["NS"], cfg["TP"]
            xb, xtok = cfg["xb"], cfg["xtok"]
            S.dma("sp", xb[0:TP, 0:NS, :], cfg["xsrc"].rearrange("(s p) d -> p s d", p=TP),
                  writes=[xtok + "#%d" % s for s in range(NS)], semkey=xtok + "_ld")
            S.phase = "A"
            for _ in rms_to_fm_gen(xb, xtok, NS, TP, "g_mix", fm[0], "fm0"):
                S.phase = cfg.get("a_phase_back", "A")
                yield
                S.phase = "A"

        def macro_tile(cfg):
            T, NS, TP, C = cfg["T"], cfg["NS"], cfg["TP"], cfg["C"]
            xb, xtok = cfg["xb"], cfg["xtok"]
            bump[0] = 0
            if not cfg.get("a_done"):
                for _ in phase_a(cfg):
                    pass
            S.phase = "B"

            rkv = aalloc("rkv", (12, T), BF16)
            lora = aalloc("lora", (2, T), F32)
            rwg = aalloc("rwg", (NS, 512), BF16)
            cfg["mark1"] = bump[0]
            hq = aalloc("hq", (4, T), BF16)
            hE = aalloc("hE", (4, T), F32)
            hi = aalloc("hi", (NS, 512), BF16)
            hsg = aalloc("hsg", (NS, 512), BF16)
            cfg["mark2"] = bump[0]
            pf = [aalloc("pf%d" % i, (1, T + 4), F32) for i in range(2)]
            dif = [aalloc("dif%d" % i, (1, T), F32) for i in range(2)]
            tg = [aalloc("tg0", (1, 512), F32)] * 2
            cnt = {"pf": 0, "tg": 0}

            def evac_rw(cbase):
                def f(m, pv, ptok):
                    cidx = cbase + m
                    i = cnt["pf"] % 2
                    cnt["pf"] += 1
                    pfb, dfb = pf[i][:, 0, :], dif[i][:, 0, :]
                    pk, dk = "pf%d" % i, "dif%d" % i
                    act(pfb[:, 1:T + 1], pv, AF.Copy, [ptok], [pk + "#m"])
                    if cfg["sample"]:
                        cp("pool", pfb[:, 0:1], shs[:, cidx:cidx + 1], ["shs"], [pk + "#c"])
                        tt("dve", dfb[:, 0:T], pfb[:, 0:T], pv, ALU.subtract, [pk + "#m", pk + "#c", ptok], [dk])
                        tt("dve", dfb[:, 32:33], shs[:, 14 + cidx:15 + cidx], pfb[:, 33:34], ALU.subtract,
                           [pk + "#m", "shs", dk], [dk])
                        cp("pool", sho[:, cidx:cidx + 1], pfb[:, 32:33], [pk + "#m"], ["sho#a%d" % cidx])
                        cp("pool", sho[:, 14 + cidx:15 + cidx], pfb[:, 64:65], [pk + "#m"], ["sho#b%d" % cidx])
                    else:
                        cp("pool", pfb[:, 0:1], carry[:, cidx:cidx + 1], ["carry#%d" % cidx], [pk + "#c"])
                        tt("dve", dfb[:, 0:T], pfb[:, 0:T], pv, ALU.subtract, [pk + "#m", pk + "#c", ptok], [dk])
                        cp("pool", carry[:, cidx:cidx + 1], pfb[:, T:T + 1], [pk + "#m", pk + "#c"], ["carry#%d" % cidx])
                    if cidx < 12:
                        dst, dtk = rkv[:, cidx, :], "rkv#%d" % cidx
                    else:
                        dst, dtk = lora[:, cidx - 12, :], "lora#%d" % (cidx - 12)
                    stt(dst, dfb[:, 0:T], vcol("mu", cidx), pv, ALU.mult, ALU.add, [dk, ptok] + VPR, [dtk])
                return f

            def evac_q(m, pv, ptok):
                cp("act", hq[:, m, :], pv, [ptok], ["hq#%d" % m])

            def evac_f(m, pv, ptok):
                act(hE[:, m, :], pv, AF.Exp, [ptok], ["hE#%d" % m], scale=-1.0)

            def evac_i(s, pv, ptok):
                cp("act", hi[0:TP, s, :], pv, [ptok], ["hi#%d" % s])

            def evac_g(s, pv, ptok):
                i = cnt["tg"] % 2
                cnt["tg"] += 1
                t1, tk = tg[i][0:TP, 0, :], "tg0"
                act(t1, pv, AF.Exp, [ptok], [tk], scale=-1.0)
                sigmoid_from_exp(t1, tk)
                tt("dve", t1, pv, t1, ALU.mult, [tk, ptok], [tk])
                o = VB_COLS["hgn"]
                tt("pool", hsg[0:TP, s, :], t1, vbb[0:TP, o:o + 512], ALU.mult, [tk, "vbb"], ["hsg#%d" % s])

            wt, wk = wload("w_in", 0, 8, 0, 512)
            proj_fm(fm[0], "fm0", wt, wk, 512, T, evac_q)
            wt, wk = wload("w_in", 0, 8, 512, 512)
            proj_fm(fm[0], "fm0", wt, wk, 512, T, evac_f)
            wt, wk = wload("w_in", 0, 8, 1024, 512)
            proj_tm(fm[0], "fm0", wt, wk, 512, NS, TP, evac_i)
            wt, wk = wload("w_in", 0, 8, 1536, 512)
            proj_tm(fm[0], "fm0", wt, wk, 512, NS, TP, evac_g)
            hgen = hgrn_mixer(cfg, hq, hE, hi, hsg)
            next(hgen)
            S.phase = "B"
            wt, wk = wload("w_in", 0, 8, 3584, 256)
            proj_fm(fm[0], "fm0", wt, wk, 256, T, evac_rw(12))
            next(hgen)
            for j in range(3):
                S.phase = "B"
                wt, wk = wload("w_in", 0, 8, 2048 + 512 * j, 512)
                proj_fm(fm[0], "fm0", wt, wk, 512, T, evac_rw(4 * j))
                next(hgen)
            dbg_store("rkv", rkv[:, :, :], ["rkv"])
            dbg_store("lora", lora[:, :, :], ["lora"])
            if stop_after == "B":
                return
            S.phase = "HG"
            next(hgen)
            dbg_store("mixhg", mixtm[0:TP, 0:NS, 0:512], ["mixtm"])
            dbg_store("S_hg", cfg["S_hg"](0)[0][:, :, :], [cfg["S_hg"](0)[2] + "f"])
            if stop_after == "HG":
                return
            S.phase = "RWprep"
            rwkv_mixer(cfg, rkv, lora, rwg)
            dbg_store("mixrw", mixtm[0:TP, 0:NS, 512:1024], ["mixtm"])
            dbg_store("H_rw", cfg["H_rw"](0)[0][:, :, :], [cfg["H_rw"](0)[2] + "f"])
            if stop_after == "RW":
                return
            S.phase = "OUT"
            tm_to_fm(lambda s: mixtm[0:TP, s, :], lambda s: ["mixtm#h%d" % s, "mixtm#r%d" % s], NS, TP, fm[1], "fm1")
            resid_proj(cfg, fm[1], "fm1", "w_out")
            dbg_store("x1", xb[0:TP, 0:NS, :], [xtok])
            if stop_after == "OUT":
                return
            S.phase = "X"
            cross_attn(cfg)
            dbg_store("x2", xb[0:TP, 0:NS, :], [xtok])
            if stop_after == "X":
                return
            S.phase = "F1"
            ffn(cfg)

        def memory_kv():
            bump[0] = 0
            mtm = aalloc("mtm", (2, D), F32)
            S.dma("sp", mtm[:, :, :], I["mem"].rearrange("(s p) d -> p s d", p=128), writes=["mtm#0", "mtm#1"], semkey="mtm_ld")
            rms_to_fm(mtm, "mtm", 2, 128, "g_mem", fm[0], "fm0")
            okv = [aalloc("okv%d" % i, (1, 512), F32) for i in range(2)]
            cnt = {"n": 0}
            _mode = int(_os.environ.get('MEMKV_MODE', '9'))
            if _mode < 2:
                return
            for (wname, oname, isk) in (("w_ck", "o_mk", True), ("w_cv", "o_mv", False)):
                if _mode < 4 and not isk:
                    continue
                for half in range(2):
                    wt, wk = wload(wname, 0, 8, half * 512, 512)

                    def ev(s, pv, ptok, half=half, oname=oname, isk=isk):
                        i = cnt["n"] % 2
                        cnt["n"] += 1
                        ob, ok = okv[i][:, 0, :], "okv%d" % i
                        cp("act", ob, pv, [ptok], [ok])
                        S.dma("act", O[oname][s * 128:(s + 1) * 128, half * 512:(half + 1) * 512], ob, reads=[ok],
                              semkey=ok + "_st", store=True)
                        if not isk:
                            cp("dve", mvtm[:, s, half * 512:(half + 1) * 512], pv, [ptok], ["mem0v"])
                    proj_tm(fm[0], "fm0", wt, wk, 512, 2, 128, ev)
                    if isk and _mode >= 3:
                        def evk(m, pv, ptok, half=half):
                            cp("act", mkfm[:, half * 4 + m, :], pv, [ptok], ["mem0k"])
                        proj_fm(fm[0], "fm0", wt, wk, 512, NMEM, evk)

        wconvert_all()
        if not _os.environ.get('SKIP_MEMKV'):
            memory_kv()
        S.i("pool", "memset", [], ["carry"], ap=carry[:], constant=0.0)
        S.i("pool", "memset", [], ["S_hg0f"], ap=S_hg[0][:], constant=0.0)
        S.i("pool", "memset", [], ["S_hg0b"], ap=S_hgb[0][:], constant=0.0)
        S.i("pool", "memset", [], ["H_rw0f"], ap=H_rw[0][:], constant=0.0)
        S.i("pool", "memset", [], ["H_rw0b"], ap=H_rwb[0][:], constant=0.0)
        cfgs = []
        for mt in range(NMT):
            cfg = dict(T=512, NS=4, TP=128, C=64, xb=xres[mt % 2], xtok="xres%d" % (mt % 2),
                       xsrc=I["xp"][mt * 512:(mt + 1) * 512, :], ydst=O["yp"][mt * 512:(mt + 1) * 512, :], sample=False)
            cfg["S_hg"] = lambda ch: (S_hg[0], S_hgb[0], "S_hg0")
            cfg["H_rw"] = lambda ch: (H_rw[0], H_rwb[0], "H_rw0")
            cfg["hf_chunks"] = (7,) if mt == NMT - 1 else ()
            cfg["att_groups"] = lambda s: [(slice(0, 128), 0)]
            cfg["mem"] = lambda mi: (mkfm, mvtm, "mem0")
            cfg["att_full"] = [(slice(0, 512), 0)]
            cfgs.append(cfg)
        for mt in range(NMT - 1):
            cfgs[mt]["next"] = cfgs[mt + 1]
        for mt in range(NMT):
            macro_tile(cfgs[mt])
        if _os.environ.get('SKIP_TAIL'):
            info = S.emit()
            return nc, info
        S.dma("act", O["o_p_hg"].rearrange("h k v -> k h v"), S_hg[0][:, :, :], reads=["S_hg0f"], semkey="o_p_hg", store=True)
        S.dma("act", O["o_p_sh"][:, :], carry[:, :], reads=["carry"], semkey="o_p_sh", store=True)

        def rw_state_out(Hf, htok, dst, tag):
            for hp in range(4):
                b = mmbank()
                S.i("pe", "transpose", [htok + "f#%d" % hp, "identf"], ["ps%d" % b], out=ps[b][:, 0:128], in_=Hf[:, hp, :],
                    identity=identf[:, :])
                so = stt_out[tag][:, hp, :]
                cp("act", so, ps[b][:, 0:128], ["ps%d" % b], ["so_%s#%d" % (tag, hp)])
                for h in range(2):
                    S.dma("act", dst[2 * hp + h, :, :], so[h * 64:(h + 1) * 64, h * 64:(h + 1) * 64],
                          reads=["so_%s#%d" % (tag, hp)], semkey="so_%s" % tag, store=True)
        bump[0] = 0
        stt_out = {"p": aalloc("so_p", (4, 128), F32)}
        rw_state_out(H_rw[0], "H_rw0", O["o_p_rw"], "p")
        if sample:
            S.dma("sp", shs[:, :], I["sh_sh"][:, :], writes=["shs"], semkey="shs_ld")
            spad = aalloc("spad", (4, 128), F32)
            for q in range(2):
                S.dma("sp", S_hg[1 + q][:, :, :], I["sh_hg"][q].rearrange("h k v -> k h v"), writes=["S_hg%df" % (1 + q)],
                      semkey="shg_ld%d" % q)
                cp("dve", S_hgb[1 + q][:, :, :], S_hg[1 + q][:, :, :], ["S_hg%df" % (1 + q)], ["S_hg%db" % (1 + q)])
                S.i("pool", "memset", [], ["spad"], ap=spad[:, :, :], constant=0.0)
                for hp in range(4):
                    for h in range(2):
                        S.dma("sp", spad[h * 64:(h + 1) * 64, hp, h * 64:(h + 1) * 64], I["sh_rw"][q, 2 * hp + h, :, :],
                              reads=[], writes=["spad#%d" % hp], semkey="srw_ld%d" % hp)
                for hp in range(4):
                    b = mmbank()
                    S.i("pe", "transpose", ["spad#%d" % hp, "identf"], ["ps%d" % b], out=ps[b][:, 0:128], in_=spad[:, hp, :],
                        identity=identf[:, :])
                    cp("act", H_rw[1 + q][:, hp, :], ps[b][:, 0:128], ["ps%d" % b], ["H_rw%df#%d" % (1 + q, hp)])
                    cp("dve", H_rwb[1 + q][:, hp, :], ps[b][:, 0:128], ["ps%d" % b], ["H_rw%db#%d" % (1 + q, hp)])
            xi = NMT % 2
            cfg = dict(T=64, NS=1, TP=64, C=32, xb=xres[xi], xtok="xres%d" % xi, xsrc=I["xs"], ydst=O["ys"], sample=True)
            cfg["S_hg"] = lambda ch: (S_hg[1 + ch], S_hgb[1 + ch], "S_hg%d" % (1 + ch))
            cfg["H_rw"] = lambda ch: (H_rw[1 + ch], H_rwb[1 + ch], "H_rw%d" % (1 + ch))
            cfg["hf_chunks"] = None
            cfg["att_groups"] = lambda s: [(slice(0, 32), 0), (slice(32, 64), 1)]
            cfg["att_full"] = [(slice(0, 32), 0), (slice(32, 64), 1)]
            macro_tile(cfg)
            bump[0] = 0
            stt_out["s0"] = aalloc("so_s0", (4, 128), F32)
            stt_out["s1"] = aalloc("so_s1", (4, 128), F32)
            for q in range(2):
                S.dma("act", O["o_s_hg"][q].rearrange("h k v -> k h v"), S_hg[1 + q][:, :, :], reads=["S_hg%df" % (1 + q)],
                      semkey="o_s_hg%d" % q, store=True)
                rw_state_out(H_rw[1 + q], "H_rw%d" % (1 + q), O["o_s_rw"][q], "s%d" % q)
            S.dma("act", O["o_s_sh"][:, :], sho[:, :], reads=["sho"], semkey="o_s_sh", store=True)
        info = S.emit()
    return nc, info


def host_inputs(inp, core, NMT=8):
    f = lambda a: np.ascontiguousarray(np.asarray(a, np.float32))
    m = {}
    m["xp"] = f(inp["x_prompt"][core][:512 * NMT])
    m["xs"] = f(inp["x_sample"][2 * core:2 * core + 2].reshape(64, D))
    m["mem"] = f(inp["mem_prompt"][core])
    m["ck"] = f(inp["cache_mem_k"][0, 2 * core:2 * core + 2].reshape(2, NMEM, D))
    m["cv"] = f(inp["cache_mem_v"][0, 2 * core:2 * core + 2].reshape(2, NMEM, D))
    m["sh_hg"] = f(inp["state_hgrn"][0, 2 * core:2 * core + 2])
    m["sh_rw"] = f(inp["state_rwkv"][0, 2 * core:2 * core + 2])
    sh = np.asarray(inp["state_rwkv_shift"], np.float32)[0, 2 * core:2 * core + 2, 0]
    m["sh_sh"] = f(np.concatenate([_chunks(sh[0]), _chunks(sh[1])], 1))
    for k in ("w_in", "w_out", "w_cq", "w_ck", "w_cv", "w_co", "w_ff1", "w_ff3", "w_ff2"):
        m[k] = f(inp[k][0])
    m["w_b"] = f(inp["rw_w_b"][0])
    m["a_b"] = f(inp["rw_a_b"][0])
    m["g_b"] = f(inp["rw_g_b"][0])
    vp = np.zeros((128, NVP_IN), np.float32)

    def put(name, v):
        c = _chunks(v)
        vp[:, VP_COLS[name]:VP_COLS[name] + c.shape[1]] = c
    put("g_mix", inp["norm_mix"][0]); put("g_cross", inp["norm_cross"][0]); put("g_ffn", inp["norm_ffn"][0])
    put("g_mem", inp["norm_mem"][0])
    put("lb0", inp["hgrn_lb_logits"][0]); put("lb1", inp["hgrn_lb_logits"][1])
    put("mu", inp["rw_mu"][0]); put("w0", inp["rw_w0"][0]); put("a0", inp["rw_a0"][0])
    put("kk", inp["rw_k_k"][0]); put("ka", inp["rw_k_a"][0]); put("rk", np.asarray(inp["rw_r_k"][0]).reshape(-1))
    m["vp"] = vp
    vb = np.zeros((1, NVB), np.float32)
    vb[0, 0:512] = inp["hgrn_norm"][0]; vb[0, 512:1024] = inp["rw_gn_w"][0]; vb[0, 1024:1536] = inp["rw_gn_b"][0]
    vb[0, 1536:2560] = inp["norm_final"]
    m["vb"] = vb
    m["ident"] = np.eye(128, dtype=np.float32)
    p = np.arange(128)
    m["e64"] = (p[:, None] // 64 == p[None, :] // 64).astype(np.float32)
    m["ehead"] = (p[:, None] // 64 == np.arange(2)[None, :]).astype(np.float32)
    r64 = np.ones((1, 512), np.float32); r64[0, ::64] = 0
    r32 = np.ones((1, 64), np.float32); r32[0, ::32] = 0
    m["rmask64"] = r64
    m["rmask32"] = r32
    for k, v in make_masks().items():
        m["m_" + k] = v
    return m


_CACHE = {}


def kernel(**inputs):
    inp = {k: np.asarray(v) for k, v in inputs.items()}
    if "nc" not in _CACHE:
        _CACHE["nc"] = build_program(NMT=8, sample=True)[0]
    nc = _CACHE["nc"]
    maps = [host_inputs(inp, c, NMT=8) for c in range(8)]
    res = run_bass_kernel_spmd(nc, maps, core_ids=list(range(8)))
    R = res.results
    f = lambda a: np.ascontiguousarray(np.asarray(a, np.float32))
    unch = lambda a: f(a).T.reshape(-1)
    y_prompt = np.stack([f(R[c]["yp"]) for c in range(8)], 0)
    y_sample = np.concatenate([f(R[c]["ys"]).reshape(2, 32, D) for c in range(8)], 0)
    p_hg = np.stack([f(R[c]["o_p_hg"]) for c in range(8)], 0)[None]
    p_rw = np.stack([f(R[c]["o_p_rw"]) for c in range(8)], 0)[None]
    p_sh = np.stack([unch(R[c]["o_p_sh"]) for c in range(8)], 0).reshape(1, 8, 1, RWC)
    p_mk = np.stack([f(R[c]["o_mk"]).reshape(NMEM, 4, 256) for c in range(8)], 0)[None]
    p_mv = np.stack([f(R[c]["o_mv"]).reshape(NMEM, 4, 256) for c in range(8)], 0)[None]
    s_hg = np.concatenate([f(R[c]["o_s_hg"]) for c in range(8)], 0)[None]
    s_rw = np.concatenate([f(R[c]["o_s_rw"]) for c in range(8)], 0)[None]
    s_sh = np.stack([unch(f(R[c]["o_s_sh"])[:, 14 * q:14 * q + 14]) for c in range(8) for q in range(2)], 0)
    s_sh = s_sh.reshape(1, 16, 1, RWC)
    return (y_prompt, y_sample, p_hg, p_rw, p_sh, p_mk, p_mv, s_hg, s_rw, s_sh)
```
